# Optimizing a Trainium2 kernel written in Bass

```python
import math
import jax, jax.numpy as jnp
from jax import lax
import numpy as np

D_MODEL = 1024
BATCH = 16
SEQ = 256
DEPTH = 4
DEC_BATCH = 4
DEC_SEQ = 1024
PAST_LEN = 256

GRID_W = 64
N_MIXERS = 2
MIX_WIDTH = D_MODEL
DA_HEADS = 8
DA_DH = 64
DA_DV = 2 * DA_DH
DA_IN = 2 * DA_HEADS * 2 * DA_DH + DA_HEADS * DA_DV + MIX_WIDTH
MLA_HEADS = 8
MLA_Q_LORA = 384
MLA_KV_LORA = 256
MLA_NOPE = 128
MLA_ROPE = 64
MLA_DV = 128
MLA_IN = MLA_Q_LORA + MLA_KV_LORA + MLA_ROPE + MIX_WIDTH
ROPE_THETA = 10000.0
NORM_EPS = 1e-6
Q_BLOCK = 128

kernel_name = 'diffmla_prefix_diffusion_step'

F32 = jnp.float32


def rmsnorm(x, g):
    xf = x.astype(F32)
    y = xf * lax.rsqrt(jnp.mean(xf * xf, axis=-1, keepdims=True) + NORM_EPS)
    return (y * g.astype(F32)).astype(x.dtype)


def axial_rope(n_tokens, dim):
    rows = n_tokens // GRID_W
    row = jnp.repeat(jnp.arange(rows), GRID_W).astype(F32)
    col = jnp.tile(jnp.arange(GRID_W), rows).astype(F32)
    n_freq = dim // 4
    inv = ROPE_THETA ** (-jnp.arange(n_freq, dtype=F32) / n_freq)
    ang = jnp.concatenate([row[:, None] * inv, col[:, None] * inv], axis=-1)
    return jnp.cos(ang), jnp.sin(ang)


def apply_rope(x, cos, sin):
    shape = (cos.shape[0],) + (1,) * (x.ndim - 3) + (cos.shape[-1],)
    cos = cos.reshape(shape)
    sin = sin.reshape(shape)
    xf = x.astype(F32)
    half = x.shape[-1] // 2
    x1, x2 = xf[..., :half], xf[..., half:]
    out = jnp.concatenate([x1 * cos - x2 * sin, x1 * sin + x2 * cos], axis=-1)
    return out.astype(x.dtype)


def ada_mod(cond, w, b):
    m = jax.nn.silu(cond) @ w + b
    if m.ndim == 2:
        m = m[:, None, :]
    shift, scale, gate = jnp.split(m, 3, axis=-1)
    return shift, scale, gate


def map_query_blocks(fn, q):
    b, s = q.shape[0], q.shape[1]
    if s % Q_BLOCK != 0 or s <= Q_BLOCK:
        return fn(q)
    nb = s // Q_BLOCK
    qb = jnp.swapaxes(q.reshape((b, nb, Q_BLOCK) + q.shape[2:]), 0, 1)
    out = jnp.swapaxes(lax.map(fn, qb), 0, 1)
    return out.reshape((b, s) + out.shape[3:])


def diff_project(h, w_in):
    b, s = h.shape[:2]
    proj = h @ w_in
    nq = DA_HEADS * 2 * DA_DH
    q, k, v, gate = jnp.split(proj, [nq, 2 * nq, 2 * nq + DA_HEADS * DA_DV], axis=-1)
    q = q.reshape(b, s, DA_HEADS, 2, DA_DH)
    k = k.reshape(b, s, DA_HEADS, 2, DA_DH)
    v = v.reshape(b, s, DA_HEADS, DA_DV)
    return q, k, v, gate


def diff_attend(q, k, v, gate, lam, lam_init, g_sub):
    scale = DA_DH ** -0.5

    def block(qb):
        s = jnp.einsum('bqhcd,bkhcd->bhcqk', qb, k, preferred_element_type=F32) * scale
        p = jax.nn.softmax(s, axis=-1)
        w = p[:, :, 0] - lam * p[:, :, 1]
        return jnp.einsum('bhqk,bkhe->bqhe', w.astype(v.dtype), v)

    o = map_query_blocks(block, q)
    o = rmsnorm(o, g_sub) * (1.0 - lam_init)
    b, sq = o.shape[:2]
    return o.reshape(b, sq, MIX_WIDTH) * jax.nn.silu(gate)


def mla_project(h, w_in, g_qa, w_qb, g_kva):
    b, s = h.shape[:2]
    proj = h @ w_in
    q_a, kv_a, k_pe, gate = jnp.split(
        proj, [MLA_Q_LORA, MLA_Q_LORA + MLA_KV_LORA, MLA_Q_LORA + MLA_KV_LORA + MLA_ROPE], axis=-1)
    q = (rmsnorm(q_a, g_qa) @ w_qb).reshape(b, s, MLA_HEADS, MLA_NOPE + MLA_ROPE)
    c_kv = rmsnorm(kv_a, g_kva)
    return q, c_kv, k_pe, gate


def mla_rope(q, k_pe, cos, sin):
    q_pe = apply_rope(q[..., MLA_NOPE:], cos, sin)
    q = jnp.concatenate([q[..., :MLA_NOPE], q_pe], axis=-1)
    return q, apply_rope(k_pe, cos, sin)


def mla_attend(q, c_kv, k_pe, gate, w_kvb):
    b, sk = c_kv.shape[:2]
    kv = (c_kv @ w_kvb).reshape(b, sk, MLA_HEADS, MLA_NOPE + MLA_DV)
    k_nope, v = kv[..., :MLA_NOPE], kv[..., MLA_NOPE:]
    k = jnp.concatenate(
        [k_nope, jnp.broadcast_to(k_pe[:, :, None, :], (b, sk, MLA_HEADS, MLA_ROPE))], axis=-1)
    scale = (MLA_NOPE + MLA_ROPE) ** -0.5

    def block(qb):
        s = jnp.einsum('bqhd,bkhd->bhqk', qb, k, preferred_element_type=F32) * scale
        p = jax.nn.softmax(s, axis=-1)
        return jnp.einsum('bhqk,bkhe->bqhe', p.astype(v.dtype), v)

    o = map_query_blocks(block, q)
    sq = o.shape[1]
    return o.reshape(b, sq, MIX_WIDTH) * jax.nn.silu(gate)


def setup_inputs(seed: int = 0) -> dict:
    key = jax.random.key(seed)
    ks = iter(jax.random.split(key, 32))
    n_diff = (DEPTH + 1) // 2
    n_mla = DEPTH // 2

    def nrm(shape, scale=1.0):
        return jax.random.normal(next(ks), shape, F32) * scale

    def gain(shape):
        return 1.0 + nrm(shape, 0.05)

    return {
        'x_prompt': nrm((BATCH, SEQ, D_MODEL)),
        'x_sample': nrm((DEC_BATCH, DEC_SEQ, D_MODEL)),
        'cache_diff_k': nrm((DEC_BATCH, n_diff, PAST_LEN, DA_HEADS, 2, DA_DH)),
        'cache_diff_v': nrm((DEC_BATCH, n_diff, PAST_LEN, DA_HEADS, DA_DV)),
        'cache_mla_ckv': nrm((DEC_BATCH, n_mla, PAST_LEN, MLA_KV_LORA)),
        'cache_mla_kpe': nrm((DEC_BATCH, n_mla, PAST_LEN, MLA_ROPE)),
        'c': nrm((DEC_BATCH, D_MODEL)),
        'c_ctx': nrm((D_MODEL,)),
        'w_ada': nrm((DEPTH, D_MODEL, 3 * D_MODEL), 0.5 * D_MODEL ** -0.5),
        'b_ada': nrm((DEPTH, 3 * D_MODEL), 0.01),
        'g_pre': gain((DEPTH, D_MODEL)),
        'g_post': gain((DEPTH, D_MODEL)),
        'w_out': nrm((DEPTH, MIX_WIDTH, D_MODEL), MIX_WIDTH ** -0.5),
        'da_w_in': nrm((n_diff, D_MODEL, DA_IN), D_MODEL ** -0.5),
        'da_lam_q1': nrm((n_diff, DA_DH), 0.1),
        'da_lam_k1': nrm((n_diff, DA_DH), 0.1),
        'da_lam_q2': nrm((n_diff, DA_DH), 0.1),
        'da_lam_k2': nrm((n_diff, DA_DH), 0.1),
        'da_g_sub': gain((n_diff, DA_DV)),
        'mla_w_in': nrm((n_mla, D_MODEL, MLA_IN), D_MODEL ** -0.5),
        'mla_g_qa': gain((n_mla, MLA_Q_LORA)),
        'mla_w_qb': nrm((n_mla, MLA_Q_LORA, MLA_HEADS * (MLA_NOPE + MLA_ROPE)), MLA_Q_LORA ** -0.5),
        'mla_g_kva': gain((n_mla, MLA_KV_LORA)),
        'mla_w_kvb': nrm((n_mla, MLA_KV_LORA, MLA_HEADS * (MLA_NOPE + MLA_DV)), MLA_KV_LORA ** -0.5),
    }


def reference(x_prompt, x_sample, cache_diff_k, cache_diff_v, cache_mla_ckv, cache_mla_kpe,
              c, c_ctx, w_ada, b_ada, g_pre, g_post, w_out,
              da_w_in, da_lam_q1, da_lam_k1, da_lam_q2, da_lam_k2, da_g_sub,
              mla_w_in, mla_g_qa, mla_w_qb, mla_g_kva, mla_w_kvb):
    x_ctx = x_prompt
    x_lat = x_sample
    s_lat = x_lat.shape[1]
    rope_da = axial_rope(s_lat, DA_DH)
    rope_mla = axial_rope(s_lat, MLA_ROPE)
    st_dk, st_dv, st_ckv, st_kpe = [], [], [], []

    for i in range(DEPTH):
        j = i // N_MIXERS
        kind = i % N_MIXERS
        sh_c, sc_c, gt_c = ada_mod(c_ctx, w_ada[i], b_ada[i])
        sh_l, sc_l, gt_l = ada_mod(c, w_ada[i], b_ada[i])
        hc = rmsnorm(x_ctx, g_pre[i]) * (1.0 + sc_c) + sh_c
        hl = rmsnorm(x_lat, g_pre[i]) * (1.0 + sc_l) + sh_l

        if kind == 0:
            lam_init = 0.8 - 0.6 * math.exp(-0.3 * i)
            lam = (jnp.exp(jnp.sum(da_lam_q1[j].astype(F32) * da_lam_k1[j].astype(F32)))
                   - jnp.exp(jnp.sum(da_lam_q2[j].astype(F32) * da_lam_k2[j].astype(F32)))
                   + lam_init)
            qc, kc, vc, gc = diff_project(hc, da_w_in[j])
            st_dk.append(kc)
            st_dv.append(vc)
            oc = diff_attend(qc, kc, vc, gc, lam, lam_init, da_g_sub[j])
            ql, kl, vl, gl = diff_project(hl, da_w_in[j])
            ql = apply_rope(ql, *rope_da)
            kl = apply_rope(kl, *rope_da)
            kl = jnp.concatenate([kl, cache_diff_k[:, j]], axis=1)
            vl = jnp.concatenate([vl, cache_diff_v[:, j]], axis=1)
            ol = diff_attend(ql, kl, vl, gl, lam, lam_init, da_g_sub[j])
        else:
            qc, ckv_c, kpe_c, gc = mla_project(hc, mla_w_in[j], mla_g_qa[j], mla_w_qb[j], mla_g_kva[j])
            st_ckv.append(ckv_c)
            st_kpe.append(kpe_c)
            oc = mla_attend(qc, ckv_c, kpe_c, gc, mla_w_kvb[j])
            ql, ckv_l, kpe_l, gl = mla_project(hl, mla_w_in[j], mla_g_qa[j], mla_w_qb[j], mla_g_kva[j])
            ql, kpe_l = mla_rope(ql, kpe_l, *rope_mla)
            ckv_all = jnp.concatenate([ckv_l, cache_mla_ckv[:, j]], axis=1)
            kpe_all = jnp.concatenate([kpe_l, cache_mla_kpe[:, j]], axis=1)
            ol = mla_attend(ql, ckv_all, kpe_all, gl, mla_w_kvb[j])

        x_ctx = x_ctx + gt_c * rmsnorm(oc @ w_out[i], g_post[i])
        x_lat = x_lat + gt_l * rmsnorm(ol @ w_out[i], g_post[i])

    state_diff_k = jnp.stack(st_dk, axis=1)
    state_diff_v = jnp.stack(st_dv, axis=1)
    state_mla_ckv = jnp.stack(st_ckv, axis=1)
    state_mla_kpe = jnp.stack(st_kpe, axis=1)
    return (x_ctx, x_lat, state_diff_k, state_diff_v, state_mla_ckv, state_mla_kpe)
```

```python
import contextlib
import math
import os
import numpy as np
import concourse.bass as bass
import concourse.mybir as mybir
from concourse.bass_utils import run_bass_kernel_spmd

F32 = mybir.dt.float32
BF16 = mybir.dt.bfloat16
AF = mybir.ActivationFunctionType
ALU = mybir.AluOpType

ENGS = ("pe", "act", "dve", "pool", "sp")
N_DMA_SEMS = 94
N_SW_SEMS = 24

D = 1024
T = 1024
NT = 8
NK = 1280
NKT = 10
EPS = 1e-6
NS = 3
NEG = -30000.0


class Prog:
    def __init__(self, nc, same_engine_sync=True):
        self.nc = nc
        self.ops = []
        self.last_writer = {}
        self.readers = {}
        self.same_engine_sync = same_engine_sync
        self.n_dma_grp = [0, 0]
        self.dma_slot_last = {}

    def op(self, eng, fn, reads=(), writes=(), dma=False):
        idx = len(self.ops)
        deps = set()
        lock = [("bklock", r[1]) for r in reads if isinstance(r, tuple) and r[0] == "bk"]
        if lock:
            writes = list(writes) + lock
        for r in reads:
            w = self.last_writer.get(r)
            if w is not None:
                deps.add(w)
        for r in writes:
            w = self.last_writer.get(r)
            if w is not None:
                deps.add(w)
            for rd in self.readers.get(r, ()):
                deps.add(rd)
        o = dict(eng=eng, fn=fn, deps=deps, dma=dma, idx=idx)
        if dma:
            grp = 0 if eng == "pool" else 1
            base, size = ((0, N_SW_SEMS), (N_SW_SEMS, N_DMA_SEMS - N_SW_SEMS))[grp]
            n = self.n_dma_grp[grp]
            slot = base + n % size
            o["slot"] = slot
            o["val"] = 16 * (n // size + 1)
            prev = self.dma_slot_last.get(slot)
            if prev is not None:
                deps.add(prev)
            self.dma_slot_last[slot] = idx
            self.n_dma_grp[grp] += 1
        deps.discard(idx)
        best = {}
        keep = set()
        for d_ in deps:
            p = self.ops[d_]
            if p["dma"]:
                keep.add(d_)
            else:
                if p["eng"] not in best or best[p["eng"]] < d_:
                    best[p["eng"]] = d_
        keep.update(best.values())
        o["deps"] = keep
        self.ops.append(o)
        for r in reads:
            self.readers.setdefault(r, []).append(idx)
        for r in writes:
            self.last_writer[r] = idx
            self.readers[r] = []
        return idx

    def emit(self):
        nc = self.nc
        ops = self.ops
        needs_signal = set()
        for o in ops:
            for d in o["deps"]:
                p = ops[d]
                if p["dma"]:
                    continue
                if p["eng"] == o["eng"] and not o["dma"]:
                    if p["eng"] == "pe" or not self.same_engine_sync:
                        continue
                needs_signal.add(d)
        ordinal = {}
        cnt = {e: 0 for e in ENGS}
        for o in ops:
            if o["idx"] in needs_signal:
                cnt[o["eng"]] += 1
                ordinal[o["idx"]] = cnt[o["eng"]]
        self.sig_counts = cnt
        with contextlib.ExitStack() as st:
            esem = {e: st.enter_context(nc.semaphore("s_" + e)) for e in ENGS}
            dsem = [st.enter_context(nc.semaphore("d_%d" % i)) for i in range(N_DMA_SEMS)]
            block = st.enter_context(nc.Block())
            per_eng = {e: [o for o in ops if o["eng"] == e] for e in ENGS}

            def run(e, engobj):
                waited = {}
                for o in per_eng[e]:
                    for d in sorted(o["deps"]):
                        p = ops[d]
                        if p["dma"]:
                            key = ("d", p["slot"])
                            val = p["val"]
                            sem = dsem[p["slot"]]
                        else:
                            if d not in ordinal:
                                continue
                            key = ("e", p["eng"])
                            val = ordinal[d]
                            sem = esem[p["eng"]]
                        if waited.get(key, 0) >= val:
                            continue
                        waited[key] = val
                        engobj.wait_ge(sem, val)
                    if o["fn"] is None:
                        continue
                    ins = o["fn"](engobj)
                    if o["dma"]:
                        ins.then_inc(dsem[o["slot"]], 16)
                    elif o["idx"] in ordinal:
                        ins.then_inc(esem[e], 1)

            block.tensor(lambda eng: run("pe", eng))
            block.scalar(lambda eng: run("act", eng))
            block.vector(lambda eng: run("dve", eng))
            block.gpsimd(lambda eng: run("pool", eng))
            block.sync(lambda eng: run("sp", eng))


def lam_init_of(i):
    return 0.8 - 0.6 * math.exp(-0.3 * i)


class _Stop(Exception):
    pass


def build_nc(NL=4, DBG=None):
    nc = bass.Bass("TRN2", target_bir_lowering=False)

    def din(name, shape):
        return nc.dram_tensor(name, list(shape), F32, kind="ExternalInput").ap()

    def dout(name, shape):
        return nc.dram_tensor(name, list(shape), F32, kind="ExternalOutput").ap()

    x_d = din("x", [T, D])
    cond_d = din("cond", [D])
    cos_d = din("cosT", [T, 32])
    sin_d = din("sinT", [T, 32])
    mb_d = din("mb", [40])
    cdk_d = din("cdk", [2, 256, 1024])
    cdv_d = din("cdv", [2, 256, 1024])
    cckv_d = din("cckv", [2, 256, 256])
    ckpe_d = din("ckpe", [2, 256, 64])
    w_ada_d = din("w_ada", [4, D, 3 * D])
    b_ada_d = din("b_ada", [4, 3 * D])
    g_pre_d = din("g_pre", [4, D])
    g_post_d = din("g_post", [4, D])
    w_out_d = din("w_out", [4, D, D])
    da_w_in_d = din("da_w_in", [2, D, 4096])
    lamv_d = din("lamv", [2, 4, 64])
    da_g_sub_d = din("da_g_sub", [2, 128])
    mla_w_in_d = din("mla_w_in", [2, D, 1728])
    mla_g_qa_d = din("mla_g_qa", [2, 384])
    mla_w_qb_d = din("mla_w_qb", [2, 384, 1536])
    mla_g_kva_d = din("mla_g_kva", [2, 256])
    mla_w_kvb_d = din("mla_w_kvb", [2, 256, 2048])

    y_d = dout("y", [T, D])
    kst_d = dout("kst", [2, T, 1024])
    vst_d = dout("vst", [2, T, 1024])
    ckvst_d = dout("ckvst", [2, T, 256])
    kpest_d = dout("kpest", [2, T, 64])

    st = contextlib.ExitStack()
    with st:
        def sb(name, shape, dt):
            return st.enter_context(nc.sbuf_tensor(name, list(shape), dt))

        xs = sb("xs", [128, NT, D], F32)
        qT = sb("qT", [128, 12, T], BF16)
        kT = sb("kT", [128, 8, NK], BF16)
        Vp = sb("Vp", [128, NKT, 8, 130], BF16)
        gate = sb("gate", [128, NT, D], BF16)
        modv = sb("modv", [128, 3 * D], F32)
        Wr = [sb("W%d" % i, [128, 8, 512], BF16) for i in range(NS)]
        ckvT = sb("ckvT", [128, 2, NK], BF16)
        kpeT = sb("kpeT", [128, NK], BF16)
        scrA = sb("scrA", [128, D], F32)
        hT = sb("hT", [128, 8, T], BF16)
        hb = sb("hb", [128, 2, D], BF16)
        q_r = sb("q_r", [128, 2, 512], BF16)
        stage = sb("stage", [128, 2, 512], F32)
        ident = sb("ident", [128, 128], BF16)
        cos2 = sb("cos2_s", [128, NT, 64], F32)
        nsin = sb("nsin_s", [128, NT, 32], F32)
        sinT = sb("sinT_s", [128, NT, 32], F32)
        mb = sb("mb_s", [128, 40], F32)
        S_rep = sb("S_rep", [128, 8, 128], BF16)
        lsm = sb("lsm", [128, 16], F32)
        neglam = sb("neglam", [128, 2], F32)
        gs_bc = sb("gs_bc", [128, 2, 128], F32)
        gqa_bc = sb("gqa_bc", [128, 384], F32)
        gkva_bc = sb("gkva_bc", [128, 256], F32)
        stat = sb("stat", [128, 80], F32)
        cm05 = sb("cm05", [128, 8], F32)
        kpe_st = sb("kpe_st", [128, 2, 64], F32)
        kpe_b = sb("kpe_b", [128, 128], BF16)
        ovb = sb("ovb", [128, 1280], F32)
        PT = ovb[:, 0:768].bitcast(BF16).rearrange("p (a n) -> p a n", a=3)
        t0 = ovb[:, 768:1280].rearrange("p (s h d) -> p s h d", s=2, h=2)
        rt = ovb[:, 0:1024].rearrange("p (a h d) -> p a h d", a=2, h=8)
        qzA = sb("qzA", [128, 2, 2, 256], BF16)
        qzB = sb("qzB", [128, 2, 2, 256], BF16)

        hTf = hT[:].rearrange("p c t -> p (c t)").bitcast(F32)
        opre = hTf[:, 0:2048].rearrange("p (a h d) -> p a h d", a=2, h=8)
        Gn = hTf[:, 2048:3072]
        land = hTf[:, 3072:4096]
        stage_flat = stage[:].rearrange("p a n -> p (a n)")
        ogT = stage_flat.bitcast(BF16).rearrange("p (c n) -> p c n", c=8)
        lamt = stage_flat[:, 0:512].rearrange("p (a b c) -> p a b c", a=2, b=4)
        identf = t0[:, 1, 0, :]
        io = t0[:, 1, 1, :].bitcast(mybir.dt.int32)
        qaT = modv[:, 0:2048].bitcast(BF16)[:, 0:3 * T].rearrange("p (c t) -> p c t", c=3)

        BK = [st.enter_context(nc.psum_tensor("bk%d" % i, [128, 512], F32)) for i in range(8)]
        TBk = {k: BK[k][:].bitcast(BF16)[:, 0:512].rearrange("p (c n) -> p c n", c=4) for k in (5, 6)}
        tb_banks = [5, 6]

        P = Prog(nc)
        op = P.op
        dram_list = []

        def dram_res():
            r = ("dram", len(dram_list))
            dram_list.append(r)
            return r

        wq = []
        wstate = dict(issued=0, taken=0)
        wfree = list(range(NS))
        slot_of = {}

        def w_issue():
            while wstate["issued"] < len(wq) and wfree:
                n = wstate["issued"]
                src, kc, ncols = wq[n]
                s = wfree.pop(0)
                slot_of[n] = s
                op("pool", lambda e, s=s, src=src, kc=kc, ncols=ncols: e.dma_start(out=Wr[s][:, 0:kc, 0:ncols], in_=src),
                   writes=[("w", s)], dma=True)
                wstate["issued"] += 1

        def w_take():
            n = wstate["taken"]
            assert n < wstate["issued"], "weight not issued"
            wstate["taken"] += 1
            return slot_of[n]

        def w_done(s):
            wfree.append(s)
            w_issue()

        def wsrc(ap2d, kc):
            return ap2d.rearrange("(c p) n -> p c n", p=128)

        for i in range(NL):
            j = i // 2
            for blk in (range(3) if i == 0 else range(6)):
                wq.append((wsrc(w_ada_d[i][:, blk * 512:(blk + 1) * 512], 8), 8, 512))
            if i % 2 == 0:
                for blk in range(8):
                    wq.append((wsrc(da_w_in_d[j][:, blk * 512:(blk + 1) * 512], 8), 8, 512))
            else:
                wq.append((wsrc(mla_w_in_d[j][:, 0:384], 8), 8, 384))
                wq.append((wsrc(mla_w_in_d[j][:, 384:704], 8), 8, 320))
                for blk in range(2):
                    wq.append((wsrc(mla_w_in_d[j][:, 704 + blk * 512:704 + (blk + 1) * 512], 8), 8, 512))
                for blk in range(3):
                    wq.append((wsrc(mla_w_qb_d[j][:, blk * 512:(blk + 1) * 512], 3), 3, 512))
                for blk in range(4):
                    wq.append((wsrc(mla_w_kvb_d[j][:, blk * 512:(blk + 1) * 512], 2), 2, 512))
            for blk in range(2):
                wq.append((wsrc(w_out_d[i][:, blk * 512:(blk + 1) * 512], 8), 8, 512))

        for hh_ in range(2):
            op("sp", lambda e, hh_=hh_: e.dma_start(out=cos2[:, :, hh_ * 32:(hh_ + 1) * 32], in_=cos_d.rearrange("(t p) f -> p t f", p=128)),
               writes=[("cos", hh_)], dma=True)
        op("sp", lambda e: e.dma_start(out=sinT[:], in_=sin_d.rearrange("(t p) f -> p t f", p=128)), writes=["sin"], dma=True)
        op("sp", lambda e: e.dma_start(out=mb[:], in_=mb_d.partition_broadcast(128)), writes=["mb"], dma=True)
        op("sp", lambda e: e.dma_start(out=scrA[:], in_=cond_d.partition_broadcast(128)), writes=[("scrA", 0), ("scrA", 1)], dma=True)
        op("sp", lambda e: e.dma_start(out=stage_flat[:, 0:512],
                                       in_=lamv_d.rearrange("a b c -> (a b c)").partition_broadcast(128)), writes=["lamt", ("stage", 0)], dma=True)
        op("sp", lambda e: e.dma_start(out=gs_bc[:].rearrange("p a b -> p (a b)"),
                                       in_=da_g_sub_d.rearrange("a b -> (a b)").partition_broadcast(128)), writes=["gs"], dma=True)
        for t in range(NT):
            op("sp", lambda e, t=t: e.dma_start(out=xs[:, t, :], in_=x_d[t * 128:(t + 1) * 128, :]),
               writes=[("x", t)], dma=True)
        w_issue()
        qTflat = qT[:].rearrange("p c t -> p (c t)")
        ada0_direct = {}
        for k_ in range(3):
            sl_ = qTflat[:, k_ * 4096:(k_ + 1) * 4096].rearrange("p (c n) -> p c n", c=8)
            ada0_direct[3 + k_] = (sl_, [("qTslot", k_)])
            op("pool", lambda e, sl_=sl_, k_=k_: e.dma_start(out=sl_, in_=wsrc(w_ada_d[0][:, (3 + k_) * 512:(4 + k_) * 512], 8)),
               writes=[("qTslot", k_)], dma=True)

        op("pool", lambda e: e.memset(cm05[:], -0.5), writes=["cm05"])
        op("pool", lambda e: e.memset(qzA[:], 0.0), writes=[("qz", 0, a_, b_) for a_ in range(2) for b_ in range(2)])
        op("pool", lambda e: e.memset(qzB[:], 0.0), writes=[("qz", 1, a_, b_) for a_ in range(2) for b_ in range(2)])
        op("dve", lambda e: e.tensor_scalar(out=nsin[:], in0=sinT[:], scalar1=-1.0, scalar2=None, op0=ALU.mult), reads=["sin"], writes=["nsin"])
        op("pool", lambda e: e.iota(io, [[1, 128]], base=0, channel_multiplier=-1), writes=["io"])
        op("dve", lambda e: e.tensor_scalar(out=identf, in0=io, scalar1=0, scalar2=None, op0=ALU.is_equal),
           reads=["io"], writes=["identf"])
        op("dve", lambda e: e.tensor_copy(out=ident[:], in_=identf), reads=["identf"], writes=["ident"])
        op("pool", lambda e: e.memset(Vp[:, :, :, 128:130], 1.0), writes=["Vp_ones"])

        for j in range(2):
            for k in range(2):
                op("dve", lambda e, j=j, k=k: e.scalar_tensor_tensor(
                    out=q_r[:, 0, 0:64], in0=lamt[:, j, 2 * k, :], scalar=1.0, in1=lamt[:, j, 2 * k + 1, :],
                    op0=ALU.mult, op1=ALU.mult, accum_out=lsm[:, 2 * j + k:2 * j + k + 1]),
                   reads=["lamt"], writes=[("lsm", j, k), ("q_r", 0)])
        op("act", lambda e: e.activation(out=lsm[:, 4:8], in_=lsm[:, 0:4], func=AF.Exp),
           reads=[("lsm", 0, 0), ("lsm", 0, 1), ("lsm", 1, 0), ("lsm", 1, 1)], writes=["lsm_e"])
        for j in range(2):
            li = lam_init_of(2 * j)
            op("dve", lambda e, j=j: e.tensor_tensor(out=lsm[:, 8 + j:9 + j], in0=lsm[:, 4 + 2 * j:5 + 2 * j],
                                                     in1=lsm[:, 5 + 2 * j:6 + 2 * j], op=ALU.subtract),
               reads=["lsm_e"], writes=[("lsm_d", j)])
            op("dve", lambda e, j=j, li=li: e.tensor_scalar(out=neglam[:, j:j + 1], in0=lsm[:, 8 + j:9 + j],
                                                          scalar1=li, scalar2=-1.0, op0=ALU.add, op1=ALU.mult),
               reads=[("lsm_d", j)], writes=[("neglam", j)])
            op("dve", lambda e, j=j, li=li: e.tensor_scalar(out=gs_bc[:, j, :], in0=gs_bc[:, j, :], scalar1=1.0 - li,
                                                          scalar2=None, op0=ALU.mult),
               reads=["gs"], writes=[("gsj", j)])

        op("act", lambda e: e.activation(out=hb[:, 0, :], in_=scrA[:], func=AF.Silu), reads=[("scrA", 0), ("scrA", 1)], writes=[("hb", 0)])
        for a in range(2):
            for c in range(4):
                op("pe", lambda e, a=a, c=c: e.transpose(out=TBk[5 + a][:, c, :], in_=hb[:, 0, (a * 4 + c) * 128:(a * 4 + c + 1) * 128],
                                                        identity=ident[:]),
                   reads=[("hb", 0), "ident"], writes=[("bk", 5 + a)])
            op("dve", lambda e, a=a: e.tensor_copy(out=S_rep[:, a * 4:(a + 1) * 4, :], in_=TBk[5 + a][:, :, :]),
               reads=[("bk", 5 + a)], writes=["S_rep"])

        OV = ["ovl"]

        bank_rr = dict(a=0, tb=0, st=0, qr=0)

        def next_bank6():
            b = bank_rr["a"] % 5
            bank_rr["a"] += 1
            return b

        def next_tb():
            a = tb_banks[bank_rr["tb"] % len(tb_banks)]
            bank_rr["tb"] += 1
            return a

        SR = [(stage[:, 0, :], ("stage", 0)), (stage[:, 1, :], ("stage", 1)),
              (scrA[:, 0:512], ("scrA", 0)), (scrA[:, 512:1024], ("scrA", 1))]

        def next_stage():
            a = bank_rr["st"] % 4
            bank_rr["st"] += 1
            return a

        def next_qr():
            a = bank_rr["qr"] % 2
            bank_rr["qr"] += 1
            return a

        def transposes_to(src_ap_fn, nblk, src_res, dst_fn, dst_res, copy_eng="dve"):
            a = next_tb()
            for i in range(nblk):
                op("pe", lambda e, a=a, i=i: e.transpose(out=TBk[a][:, i, :], in_=src_ap_fn(i), identity=ident[:]),
                   reads=list(src_res) + ["ident"], writes=[("bk", a)])
            if copy_eng == "act":
                op("act", lambda e, a=a: e.activation(out=dst_fn(), in_=TBk[a][:, 0:nblk, :], func=AF.Identity),
                   reads=[("bk", a)], writes=list(dst_res))
            else:
                op("dve", lambda e, a=a: e.tensor_copy(out=dst_fn(), in_=TBk[a][:, 0:nblk, :]),
                   reads=[("bk", a)], writes=list(dst_res))

        def rope_block(src3, nh, t, dst3, src_res, dst_res, eng_mul="dve", eng_add="dve"):
            cb = cos2[:, t, :].unsqueeze(1).broadcast_to([128, nh, 64])
            sn = sinT[:, t, :].unsqueeze(1).broadcast_to([128, nh, 32])
            ns = nsin[:, t, :].unsqueeze(1).broadcast_to([128, nh, 32])
            ra = rt[:, 0, 0:nh, :]
            rb = rt[:, 1, 0:nh, :]
            op(eng_mul, lambda e: e.tensor_tensor(out=ra, in0=src3, in1=cb, op=ALU.mult),
               reads=list(src_res) + [("cos", 0), ("cos", 1)] + OV, writes=[("rt", 0)])
            op(eng_mul, lambda e: e.tensor_tensor(out=rb[:, :, 0:32], in0=src3[:, :, 32:64], in1=ns, op=ALU.mult),
               reads=list(src_res) + ["nsin"] + OV, writes=[("rt", 1)])
            op(eng_mul, lambda e: e.tensor_tensor(out=rb[:, :, 32:64], in0=src3[:, :, 0:32], in1=sn, op=ALU.mult),
               reads=list(src_res) + ["sin"] + OV, writes=[("rt", 2)])
            op(eng_add, lambda e: e.tensor_tensor(out=dst3, in0=ra, in1=rb, op=ALU.add),
               reads=[("rt", 0), ("rt", 1), ("rt", 2)] + OV, writes=list(dst_res))

        def barrier(tag):
            op("pool", lambda e: e.memset(stat[:, 63:64], 0.0), reads=[], writes=["ovl"] + [("hT", t_) for t_ in range(NT)])

        def chk(level):
            if DBG is not None and level >= DBG:
                raise _Stop()

        landb = land.bitcast(BF16).rearrange("p (a n) -> p a n", a=2)

        def cache_loads(i, stg, sres):
            j = i // 2
            if i % 2 == 0:
                op("pool", lambda e: e.dma_start(out=stg, in_=cdk_d[j].rearrange("(a p) n -> p a n", p=128)),
                   reads=OV, writes=list(sres), dma=True)
            else:
                op("pool", lambda e: e.dma_start(out=stg[:, 0, 0:512].rearrange("p (a n) -> p a n", a=2),
                                                 in_=cckv_d[j].rearrange("(a p) n -> p a n", p=128)),
                   reads=OV, writes=list(sres), dma=True)
                for dup in range(2):
                    op("pool", lambda e, dup=dup: e.dma_start(
                        out=stg[:, 1, 0:256].rearrange("p (a n) -> p a n", a=2)[:, :, dup * 64:(dup + 1) * 64],
                        in_=ckpe_d[j].rearrange("(a p) n -> p a n", p=128)),
                       reads=OV, writes=list(sres), dma=True)

        def cache_transposes(i, stg, sres):
            rr = list(sres) + OV
            if i % 2 == 0:
                for a in range(2):
                    for g in range(2):
                        transposes_to(lambda ii, a=a, g=g: stg[:, a, (g * 4 + ii) * 128:(g * 4 + ii + 1) * 128], 4, rr,
                                      lambda a=a, g=g: kT[:, g * 4:(g + 1) * 4, T + a * 128:T + (a + 1) * 128], [("kT", 8 + a)])
            else:
                for a in range(2):
                    transposes_to(lambda ii, a=a: stg[:, 0, a * 256 + ii * 128:a * 256 + (ii + 1) * 128], 2, rr,
                                  lambda a=a: ckvT[:, 0:2, T + a * 128:T + (a + 1) * 128], [("ckvT", 8 + a)])
                for a in range(2):
                    transposes_to(lambda ii, a=a: stg[:, 1, a * 128:(a + 1) * 128], 1, rr,
                                  lambda a=a: kpeT[:, T + a * 128:T + (a + 1) * 128].unsqueeze(1), [("kpeT", 8 + a)])

        def cache_v(i):
            j = i // 2
            if i % 2 == 0:
                for a in range(2):
                    op("pool", lambda e, a=a: e.dma_start(out=Vp[:, 8 + a, :, 0:128],
                                                          in_=cdv_d[j][a * 128:(a + 1) * 128, :].rearrange("p (h d) -> p h d", h=8)),
                       writes=[("Vp", 8 + a)], dma=True)

        def ada_pieces(i, dG, dG_res, lpre, lpre_res, lpost, lpost_res, bank, extra, split=False, direct=None):
            pcs = []

            def p_bias():
                op("sp", lambda e: e.dma_start(out=modv[:, 0:2 * D], in_=b_ada_d[i][0:2 * D].partition_broadcast(128)),
                   reads=list(extra), writes=["mBA"], dma=True)
                op("sp", lambda e: e.dma_start(out=dG, in_=b_ada_d[i][2 * D:3 * D].partition_broadcast(128)),
                   reads=list(extra), writes=list(dG_res), dma=True)
            pcs.append((p_bias, 0) if split else p_bias)

            slot_mem = {}

            def p_blk(blk, part=None):
                is_direct = direct is not None and blk in direct
                if is_direct:
                    wap, wres = direct[blk]
                else:
                    if part in (None, 0):
                        slot_mem[blk] = w_take()
                    s = slot_mem[blk]
                    wap, wres = Wr[s], [("w", s)]
                crange = range(8) if part is None else range(4 * part, 4 * part + 4)
                for c in crange:
                    op("pe", lambda e, wap=wap, c=c: e.matmul(BK[bank][:], lhsT=S_rep[:, c, :], rhs=wap[:, c, :], start=(c == 0), stop=(c == 7)),
                       reads=["S_rep"] + list(wres), writes=[("bk", bank)])
                if part == 0:
                    return
                if blk < 4:
                    dst, dres = modv[:, blk * 512:(blk + 1) * 512], ["mBA"]
                else:
                    dst, dres = dG[:, (blk - 4) * 512:(blk - 3) * 512], list(dG_res)
                op("dve", lambda e: e.tensor_tensor(out=dst, in0=BK[bank][:], in1=dst, op=ALU.add),
                   reads=[("bk", bank)] + dres + list(extra), writes=dres)
                if not is_direct:
                    w_done(slot_mem[blk])
            for blk in range(6):
                if split:
                    pcs.append((lambda blk=blk: p_blk(blk, 0), 2))
                    pcs.append((lambda blk=blk: p_blk(blk, 1), 3))
                else:
                    pcs.append(lambda blk=blk: p_blk(blk))

            def p_pre():
                op("sp", lambda e: e.dma_start(out=lpre, in_=g_pre_d[i].partition_broadcast(128)), reads=list(extra), writes=list(lpre_res), dma=True)
                op("dve", lambda e: e.scalar_tensor_tensor(out=modv[:, D:2 * D], in0=modv[:, D:2 * D], scalar=1.0, in1=lpre,
                                                           op0=ALU.add, op1=ALU.mult), reads=["mBA"] + list(lpre_res) + list(extra), writes=["mBA"])
            pcs.append((p_pre, 0) if split else p_pre)

            def p_post():
                op("sp", lambda e: e.dma_start(out=lpost, in_=g_post_d[i].partition_broadcast(128)), reads=list(extra), writes=list(lpost_res), dma=True)
                op("dve", lambda e: e.tensor_tensor(out=dG, in0=dG, in1=lpost, op=ALU.mult),
                   reads=list(dG_res) + list(lpost_res) + list(extra), writes=list(dG_res))
            pcs.append((p_post, 0) if split else p_post)
            return pcs

        def layer(i):
            j = i // 2
            chk(0)
            kind = i % 2
            last = (i == NL - 1)

            for t in range(NT):
                op("act", lambda e, t=t: e.activation(out=hb[:, 1, :], in_=xs[:, t, :], func=AF.Square, accum_out=stat[:, t:t + 1]),
                   reads=[("x", t)], writes=[("hb", 1), ("ssx", t)])
            op("pool", lambda e: e.tensor_scalar(out=stat[:, 8:16], in0=stat[:, 0:8], scalar1=1.0 / D, scalar2=EPS, op0=ALU.mult, op1=ALU.add),
               reads=[("ssx", t) for t in range(NT)], writes=["msx"])
            op("pool", lambda e: e.tensor_tensor(out=stat[:, 16:24], in0=stat[:, 8:16], in1=cm05[:, 0:8], op=ALU.pow),
               reads=["msx", "cm05"], writes=["rsx"])
            if kind == 1:
                op("sp", lambda e, j=j: e.dma_start(out=gqa_bc[:], in_=mla_g_qa_d[j].partition_broadcast(128)), writes=["gqa"], dma=True)
                op("sp", lambda e, j=j: e.dma_start(out=gkva_bc[:], in_=mla_g_kva_d[j].partition_broadcast(128)), writes=["gkva"], dma=True)
            if i == 0:
                for pc in ada_pieces(0, modv[:, 2 * D:3 * D], ["mG"], scrA[:], [("scrA", 0), ("scrA", 1)],
                                     stage_flat, [("stage", 0), ("stage", 1)], 7, [], direct=ada0_direct):
                    pc()
                op("pool", lambda e: e.memset(stat[:, 62:63], 0.0), reads=[("qTslot", k_) for k_ in range(3)],
                   writes=[("qT", t_) for t_ in range(NT)])
            else:
                op("dve", lambda e: e.tensor_copy(out=modv[:, 2 * D:3 * D], in_=Gn), reads=["Gn"] + OV, writes=["mG"])
                tb_banks[:] = [6]
                cache_transposes(i, landb, ["land"])

            chk(1)
            barrier("A")
            tb_banks[:] = [5, 6]
            if i == 0:
                cache_loads(i, hb[:], [("hb", 0), ("hb", 1)])
                cache_transposes(i, hb[:], [("hb", 0), ("hb", 1)])
            cache_v(i)

            chk(2)
            def hT_transposes(t):
                pb = t % 2
                for g in range(2):
                    transposes_to(lambda ii, pb=pb, g=g: hb[:, pb, (g * 4 + ii) * 128:(g * 4 + ii + 1) * 128], 4, [("hb", pb)],
                                  lambda t=t, g=g: hT[:, g * 4:(g + 1) * 4, t * 128:(t + 1) * 128], [("hT", t)], copy_eng="act")

            for t in range(NT):
                pb = t % 2
                op("dve", lambda e, t=t: e.scalar_tensor_tensor(out=scrA[:], in0=xs[:, t, :], scalar=stat[:, 16 + t:17 + t],
                                                               in1=modv[:, D:2 * D], op0=ALU.mult, op1=ALU.mult),
                   reads=[("x", t), "rsx", "mBA"], writes=[("scrA", 0), ("scrA", 1)])
                op("dve", lambda e, pb=pb: e.tensor_tensor(out=hb[:, pb, :], in0=scrA[:], in1=modv[:, 0:D], op=ALU.add),
                   reads=[("scrA", 0), ("scrA", 1), "mBA"] + OV, writes=[("hb", pb)])
                if t >= 1:
                    hT_transposes(t - 1)
            hT_transposes(NT - 1)

            chk(3)

            dq = []
            blk_counter = [0]

            def run_deferred(all_=False):
                while dq and (all_ or dq[0][0] <= blk_counter[0]):
                    dq.pop(0)[1]()

            def proj_block(s, t, ncols, kc=8, lhs=None, lhs_res=None):
                blk_counter[0] += 1
                b = next_bank6()
                for c in range(kc):
                    l = (hT if lhs is None else lhs)
                    op("pe", lambda e, b=b, c=c, l=l: e.matmul(BK[b][:, 0:ncols], lhsT=l[:, c, t * 128:(t + 1) * 128],
                                                              rhs=Wr[s][:, c, 0:ncols], start=(c == 0), stop=(c == kc - 1)),
                       reads=[("w", s)] + ([("hT", t)] if lhs is None else [(lhs_res, t), "mBA"]), writes=[("bk", b)])
                run_deferred()
                return b

            def q_like_evac(b, t, chunk0, rope, dst, dres, src=None, src_res=None):
                qa = next_qr()
                if rope:
                    sap = BK[b][:] if src is None else src
                    sres = [("bk", b)] if src is None else list(src_res)
                    rope_block(sap.rearrange("p (h d) -> p h d", d=64), 8, t,
                               q_r[:, qa, :].rearrange("p (h d) -> p h d", d=64), sres, [("q_r", qa)] + OV)
                else:
                    op("act", lambda e, b=b, qa=qa: e.activation(out=q_r[:, qa, :], in_=BK[b][:], func=AF.Identity),
                       reads=[("bk", b)] + OV, writes=[("q_r", qa)])
                dq.append((blk_counter[0] + 2, lambda qa=qa: transposes_to(
                    lambda ii, qa=qa: q_r[:, qa, ii * 128:(ii + 1) * 128], 4, [("q_r", qa)],
                    lambda: dst[:, chunk0:chunk0 + 4, t * 128:(t + 1) * 128], [(dres, t)], copy_eng="act")))

            def state_out(b, ncols, c0, dst_dram_fn, also=None):
                sa = next_stage()
                op("act", lambda e, b=b, sa=sa: e.activation(out=SR[sa][0][:, 0:ncols], in_=BK[b][:, c0:c0 + ncols], func=AF.Identity),
                   reads=[("bk", b)] + OV, writes=[SR[sa][1]])
                op("sp", lambda e, sa=sa: e.dma_start(out=dst_dram_fn(), in_=SR[sa][0][:, 0:ncols]),
                   reads=[SR[sa][1]], writes=[dram_res()], dma=True)
                return sa

            if kind == 0:
                for cb in range(8):
                    if DBG is not None and DBG >= 30 and cb >= DBG - 30:
                        raise _Stop()
                    s = w_take()
                    for t in range(NT):
                        b = proj_block(s, t, 512)
                        if cb < 2:
                            sa = next_stage()
                            op("act", lambda e, b=b, sa=sa: e.activation(out=SR[sa][0], in_=BK[b][:], func=AF.Identity),
                               reads=[("bk", b)] + OV, writes=[SR[sa][1]])
                            q_like_evac(b, t, cb * 4, True, qT, "qT", src=SR[sa][0], src_res=[SR[sa][1]])
                        elif cb < 4:
                            sa = state_out(b, 512, 0, lambda t=t, cb=cb: kst_d[j][t * 128:(t + 1) * 128, (cb - 2) * 512:(cb - 1) * 512])
                            q_like_evac(b, t, (cb - 2) * 4, True, kT, "kT", src=SR[sa][0], src_res=[SR[sa][1]])
                        elif cb < 6:
                            sa = state_out(b, 512, 0, lambda t=t, cb=cb: vst_d[j][t * 128:(t + 1) * 128, (cb - 4) * 512:(cb - 3) * 512])
                            op("dve", lambda e, sa=sa, t=t, cb=cb: e.tensor_copy(
                                out=Vp[:, t, (cb - 4) * 4:(cb - 3) * 4, 0:128], in_=SR[sa][0].rearrange("p (h d) -> p h d", d=128)),
                               reads=[SR[sa][1]], writes=[("Vp", t)])
                        else:
                            op("act", lambda e, b=b, t=t, cb=cb: e.activation(out=gate[:, t, (cb - 6) * 512:(cb - 5) * 512], in_=BK[b][:], func=AF.Silu),
                               reads=[("bk", b)], writes=[("gate", t, cb - 6)])
                            op("dve", lambda e, t=t, cb=cb: e.tensor_tensor(
                                out=gate[:, t, (cb - 6) * 512:(cb - 5) * 512].rearrange("p (h d) -> p h d", d=128),
                                in0=gate[:, t, (cb - 6) * 512:(cb - 5) * 512].rearrange("p (h d) -> p h d", d=128),
                                in1=gs_bc[:, j, :].unsqueeze(1).broadcast_to([128, 4, 128]), op=ALU.mult),
                               reads=[("gate", t, cb - 6), ("gsj", j)], writes=[("gate", t, cb - 6)])
                    w_done(s)
                run_deferred(True)
            else:
                s0 = w_take()
                s1 = w_take()
                def mla_T(t):
                    pb = t % 2
                    transposes_to(lambda ii, pb=pb: hb[:, pb, ii * 128:(ii + 1) * 128], 3, [("hb", pb)],
                                  lambda t=t: qaT[:, 0:3, t * 128:(t + 1) * 128], [("qaT", t), "mBA"])
                    a = next_tb()
                    for ii in range(3):
                        op("pe", lambda e, a=a, ii=ii, pb=pb: e.transpose(out=TBk[a][:, ii, :], in_=hb[:, pb, 384 + ii * 128:384 + (ii + 1) * 128],
                                                                        identity=ident[:]),
                           reads=[("hb", pb), "ident"], writes=[("bk", a)])
                    op("dve", lambda e, a=a, t=t: e.tensor_copy(out=ckvT[:, 0:2, t * 128:(t + 1) * 128], in_=TBk[a][:, 0:2, :]),
                       reads=[("bk", a)], writes=[("ckvT", t)])
                    op("dve", lambda e, a=a, t=t: e.tensor_copy(out=kpeT[:, t * 128:(t + 1) * 128], in_=TBk[a][:, 2, :]),
                       reads=[("bk", a)], writes=[("kpeT", t)])

                for t in range(NT):
                    b0 = proj_block(s0, t, 384)
                    b1 = proj_block(s1, t, 320)
                    if t >= 1:
                        mla_T(t - 1)
                    sb0 = 56 if t % 2 == 0 else 72
                    op("act", lambda e, b0=b0: e.activation(out=q_r[:, 1, 0:384], in_=BK[b0][:, 0:384], func=AF.Square, accum_out=stat[:, sb0:sb0 + 1]),
                       reads=[("bk", b0)], writes=[("q_r", 1), ("ss_qa", t % 2)])
                    op("act", lambda e, b1=b1: e.activation(out=q_r[:, 1, 0:256], in_=BK[b1][:, 0:256], func=AF.Square, accum_out=stat[:, sb0 + 1:sb0 + 2]),
                       reads=[("bk", b1)], writes=[("q_r", 1), ("ss_kv", t % 2)])
                    op("pool", lambda e: e.tensor_scalar(out=stat[:, sb0 + 2:sb0 + 3], in0=stat[:, sb0:sb0 + 1], scalar1=1.0 / 384, scalar2=EPS, op0=ALU.mult, op1=ALU.add),
                       reads=[("ss_qa", t % 2)], writes=[("ms_qa", t % 2)])
                    op("pool", lambda e: e.tensor_scalar(out=stat[:, sb0 + 3:sb0 + 4], in0=stat[:, sb0 + 1:sb0 + 2], scalar1=1.0 / 256, scalar2=EPS, op0=ALU.mult, op1=ALU.add),
                       reads=[("ss_kv", t % 2)], writes=[("ms_kv", t % 2)])
                    op("pool", lambda e: e.tensor_tensor(out=stat[:, sb0 + 4:sb0 + 6], in0=stat[:, sb0 + 2:sb0 + 4], in1=cm05[:, 0:2], op=ALU.pow),
                       reads=[("ms_qa", t % 2), ("ms_kv", t % 2), "cm05"], writes=[("rs_qk", t % 2)])
                    pb = t % 2
                    op("dve", lambda e, b0=b0, pb=pb: e.scalar_tensor_tensor(out=hb[:, pb, 0:384], in0=BK[b0][:, 0:384], scalar=stat[:, sb0 + 4:sb0 + 5],
                                                                            in1=gqa_bc[:], op0=ALU.mult, op1=ALU.mult),
                       reads=[("bk", b0), ("rs_qk", t % 2), "gqa"] + OV, writes=[("hb", pb)])
                    sa = next_stage()
                    op("dve", lambda e, b1=b1, sa=sa: e.scalar_tensor_tensor(out=SR[sa][0][:, 0:256], in0=BK[b1][:, 0:256], scalar=stat[:, sb0 + 5:sb0 + 6],
                                                                            in1=gkva_bc[:], op0=ALU.mult, op1=ALU.mult),
                       reads=[("bk", b1), ("rs_qk", t % 2), "gkva"] + OV, writes=[SR[sa][1]])
                    op("sp", lambda e, sa=sa, t=t: e.dma_start(out=ckvst_d[j][t * 128:(t + 1) * 128, :], in_=SR[sa][0][:, 0:256]),
                       reads=[SR[sa][1]], writes=[dram_res()], dma=True)
                    op("dve", lambda e, sa=sa, pb=pb: e.tensor_copy(out=hb[:, pb, 384:640], in_=SR[sa][0][:, 0:256]),
                       reads=[SR[sa][1]], writes=[("hb", pb)])
                    kp = t % 2
                    op("act", lambda e, b1=b1, kp=kp: e.activation(out=kpe_st[:, kp, :], in_=BK[b1][:, 256:320], func=AF.Identity),
                       reads=[("bk", b1)], writes=[("kpe_st", kp)])
                    op("sp", lambda e, kp=kp, t=t: e.dma_start(out=kpest_d[j][t * 128:(t + 1) * 128, :], in_=kpe_st[:, kp, :]),
                       reads=[("kpe_st", kp)], writes=[dram_res()], dma=True)
                    rope_block(kpe_st[:, kp, :].unsqueeze(1), 1, t, hb[:, pb, 640:704].unsqueeze(1), [("kpe_st", kp)], [("hb", pb)], eng_mul="pool", eng_add="pool")
                    op("pool", lambda e, pb=pb: e.tensor_copy(out=hb[:, pb, 704:768], in_=hb[:, pb, 640:704]),
                       reads=[("hb", pb)], writes=[("hb", pb)])
                mla_T(NT - 1)
                w_done(s0)
                w_done(s1)
                for cb in range(2):
                    s = w_take()
                    for t in range(NT):
                        b = proj_block(s, t, 512)
                        op("act", lambda e, b=b, t=t, cb=cb: e.activation(out=gate[:, t, cb * 512:(cb + 1) * 512], in_=BK[b][:], func=AF.Silu),
                           reads=[("bk", b)], writes=[("gate", t, cb)])
                    w_done(s)
                for cb in range(3):
                    s = w_take()
                    for t in range(NT):
                        b = proj_block(s, t, 512, kc=3, lhs=qaT, lhs_res="qaT")
                        if cb == 2:
                            sa = next_stage()
                            op("act", lambda e, b=b, sa=sa: e.activation(out=SR[sa][0], in_=BK[b][:], func=AF.Identity),
                               reads=[("bk", b)] + OV, writes=[SR[sa][1]])
                            q_like_evac(b, t, cb * 4, True, qT, "qT", src=SR[sa][0], src_res=[SR[sa][1]])
                        else:
                            q_like_evac(b, t, cb * 4, False, qT, "qT")
                    w_done(s)
                run_deferred(True)
                kblocks = [(0, 512), (512, 512), (1024, 256)]
                for half in range(2):
                    s = w_take()
                    for hh in range(4):
                        h = half * 4 + hh
                        for (k0, kn) in kblocks:
                            b = next_bank6()
                            for c in range(2):
                                op("pe", lambda e, b=b, c=c, s=s, hh=hh, k0=k0, kn=kn: e.matmul(
                                    BK[b][:, 0:kn], lhsT=Wr[s][:, c, hh * 128:(hh + 1) * 128], rhs=ckvT[:, c, k0:k0 + kn],
                                    start=(c == 0), stop=(c == 1)),
                                   reads=[("w", s)] + [("ckvT", tt) for tt in range(k0 // 128, (k0 + kn) // 128)], writes=[("bk", b)])
                            op("dve", lambda e, b=b, h=h, k0=k0, kn=kn: e.tensor_copy(out=kT[:, h, k0:k0 + kn], in_=BK[b][:, 0:kn]),
                               reads=[("bk", b)], writes=[("kT", tt) for tt in range(k0 // 128, (k0 + kn) // 128)])
                    w_done(s)
                for vb in range(2):
                    s = w_take()
                    for kt in range(NKT):
                        b = next_bank6()
                        for c in range(2):
                            op("pe", lambda e, b=b, c=c, s=s, kt=kt: e.matmul(
                                BK[b][:], lhsT=ckvT[:, c, kt * 128:(kt + 1) * 128], rhs=Wr[s][:, c, :], start=(c == 0), stop=(c == 1)),
                               reads=[("w", s), ("ckvT", kt)], writes=[("bk", b)])
                        op("act", lambda e, b=b, kt=kt, vb=vb: e.activation(
                            out=Vp[:, kt, vb * 4:(vb + 1) * 4, 0:128], in_=BK[b][:].rearrange("p (h d) -> p h d", d=128), func=AF.Identity),
                           reads=[("bk", b)], writes=[("Vp", kt)])
                    w_done(s)

            chk(4)
            so = [w_take(), w_take()]
            tb_banks[:] = [6]
            barrier("B")

            scale = 0.125 if kind == 0 else (192.0 ** -0.5)
            if kind == 0:
                units = [(qb, (2 * p, 2 * p + 1), c) for qb in range(4) for p in range(4) for c in range(2)]
            else:
                units = [(qb, hp, 0) for qb in range(4) for hp in ((0, 2), (1, 3), (4, 6), (5, 7))]
            steps = [(u, kt) for u in range(len(units)) for kt in range(NKT)]
            LA = 2
            pending = {}
            SB = (0, 1, 7)

            def qz_of(h, c):
                r0 = (c * 64) if kind == 0 else ((h % 2) * 64)
                return ((qzA, 0) if r0 == 0 else (qzB, 1))

            def fill_qz(u):
                if u >= len(units):
                    return
                qb, heads, c = units[u]
                pp = u % 2
                q0 = qb * 256
                for sidx, h in enumerate(heads):
                    qz, zi = qz_of(h, c)
                    r0 = 0 if zi == 0 else 64
                    chunk = h if kind == 0 else 8 + h // 2
                    op("dve", lambda e, qz=qz, r0=r0, sidx=sidx, chunk=chunk: e.tensor_copy(
                        out=qz[r0:r0 + 64, pp, sidx, :], in_=qT[r0:r0 + 64, chunk, q0:q0 + 256]),
                       reads=[("qT", 2 * qb), ("qT", 2 * qb + 1)], writes=[("qz", zi, pp, sidx)])

            def S_step(si):
                u, kt = steps[si]
                qb, heads, c = units[u]
                sbk = SB[si % 3]
                sres = ("bk", sbk)
                q0 = qb * 256
                qres = [("qT", 2 * qb), ("qT", 2 * qb + 1)]
                pp = u % 2
                for sidx, h in enumerate(heads):
                    Sap = BK[sbk][:, sidx * 256:(sidx + 1) * 256]
                    qz, zi = qz_of(h, c)
                    zres = ("qz", zi, pp, sidx)
                    if kind == 0:
                        op("pe", lambda e, Sap=Sap, h=h, qz=qz, sidx=sidx: e.matmul(Sap, lhsT=kT[:, h, kt * 128:(kt + 1) * 128],
                                                                                   rhs=qz[:, pp, sidx, :], start=True, stop=True),
                           reads=[("kT", kt), zres], writes=[sres])
                    else:
                        op("pe", lambda e, Sap=Sap, h=h: e.matmul(Sap, lhsT=kT[:, h, kt * 128:(kt + 1) * 128], rhs=qT[:, h, q0:q0 + 256],
                                                                  start=True, stop=False),
                           reads=[("kT", kt)] + qres, writes=[sres])
                        op("pe", lambda e, Sap=Sap, qz=qz, sidx=sidx: e.matmul(Sap, lhsT=kpeT[:, kt * 128:(kt + 1) * 128],
                                                                              rhs=qz[:, pp, sidx, :], start=False, stop=True),
                           reads=[("kpeT", kt), zres], writes=[sres])
                pi = si % 3
                op("act", lambda e: e.activation(out=PT[:, pi, :], in_=BK[sbk][:], func=AF.Exp, bias=mb[:, qb * 10 + kt:qb * 10 + kt + 1], scale=scale),
                   reads=[sres, "mb"] + OV, writes=[("PT", pi)])

            def PV_step(si):
                u, kt = steps[si]
                qb, heads, c = units[u]
                pi = si % 3
                for sidx, h in enumerate(heads):
                    ob = 2 + 2 * (u % 2) + sidx
                    for half in range(2):
                        op("pe", lambda e, half=half, ob=ob, h=h, sidx=sidx: e.matmul(
                            BK[ob][:, half * 256:half * 256 + 129],
                            lhsT=PT[:, pi, sidx * 256 + half * 128:sidx * 256 + (half + 1) * 128],
                            rhs=Vp[:, kt, h, 0:129], start=(kt == 0 and half == 0),
                            stop=(kt == NKT - 1 and half == 1), skip_group_check=True),
                           reads=[("PT", pi), ("Vp", kt), "Vp_ones"], writes=[("bk", ob)])

            def unit_epilogue(u):
                qb, heads, c = units[u]
                for sidx, h in enumerate(heads):
                    ob = 2 + 2 * (u % 2) + sidx
                    rc0 = 32 + 2 * sidx
                    op("dve", lambda e, ob=ob, rc0=rc0: e.reciprocal(
                        out=stat[:, rc0:rc0 + 2], in_=BK[ob][:].rearrange("p (a c) -> p a c", a=2)[:, :, 128]),
                       reads=[("bk", ob)], writes=[("rc", sidx)])
                    if kind == 0 and c == 1:
                        op("dve", lambda e, rc0=rc0: e.tensor_scalar(out=stat[:, rc0:rc0 + 2], in0=stat[:, rc0:rc0 + 2],
                                                                    scalar1=neglam[:, j:j + 1], scalar2=None, op0=ALU.mult),
                           reads=[("rc", sidx), ("neglam", j)], writes=[("rc", sidx)])
                    for half in range(2):
                        oc = half * 256
                        if kind == 0 and c == 0:
                            op("dve", lambda e, ob=ob, rc0=rc0, sidx=sidx, half=half, oc=oc: e.tensor_scalar(
                                out=t0[:, sidx, half, :], in0=BK[ob][:, oc:oc + 128], scalar1=stat[:, rc0 + half:rc0 + half + 1],
                                scalar2=None, op0=ALU.mult),
                               reads=[("bk", ob), ("rc", sidx)] + OV, writes=[("t0", sidx, half)])
                        elif kind == 0:
                            op("dve", lambda e, ob=ob, rc0=rc0, sidx=sidx, half=half, oc=oc, h=h: e.scalar_tensor_tensor(
                                out=opre[:, half, h, :], in0=BK[ob][:, oc:oc + 128], scalar=stat[:, rc0 + half:rc0 + half + 1],
                                in1=t0[:, sidx, half, :], op0=ALU.mult, op1=ALU.add),
                               reads=[("bk", ob), ("rc", sidx), ("t0", sidx, half)] + OV, writes=[("opre", half, h)])
                            op("dve", lambda e, half=half, h=h: e.scalar_tensor_tensor(
                                out=q_r[:, 0, 0:128], in0=opre[:, half, h, :], scalar=1.0, in1=opre[:, half, h, :],
                                op0=ALU.mult, op1=ALU.mult, accum_out=stat[:, 40 + half * 8 + h:41 + half * 8 + h]),
                               reads=[("opre", half, h)], writes=[("q_r", 0), ("ssq", half, h)])
                        else:
                            op("dve", lambda e, ob=ob, rc0=rc0, half=half, oc=oc, h=h: e.tensor_scalar(
                                out=opre[:, half, h, :], in0=BK[ob][:, oc:oc + 128], scalar1=stat[:, rc0 + half:rc0 + half + 1],
                                scalar2=None, op0=ALU.mult),
                               reads=[("bk", ob), ("rc", sidx)] + OV, writes=[("opre", half, h)])

            def tail_stage1(qb):
                pcs = []
                for half in range(2):
                    t = 2 * qb + half
                    if kind == 0:
                        c0 = 24 + 4 * half
                        rcol = 64 + 8 * half
                        op("pool", lambda e, half=half, rcol=rcol: e.tensor_scalar(out=stat[:, rcol:rcol + 8],
                                                                                 in0=stat[:, 40 + half * 8:48 + half * 8],
                                                                                 scalar1=1.0 / 128, scalar2=EPS, op0=ALU.mult, op1=ALU.add),
                           reads=[("ssq", half, hh) for hh in range(8)], writes=[("s8", half)])
                        op("pool", lambda e, rcol=rcol: e.tensor_tensor(out=stat[:, rcol:rcol + 8], in0=stat[:, rcol:rcol + 8], in1=cm05[:, 0:8], op=ALU.pow),
                           reads=[("s8", half), "cm05"], writes=[("s8", half)])
                        o3 = opre[:, half, :, :]
                        pcs.append(lambda half=half, o3=o3, rcol=rcol: op(
                            "dve", lambda e: e.tensor_tensor(out=o3, in0=o3, in1=stat[:, rcol:rcol + 8].unsqueeze(2).broadcast_to([128, 8, 128]),
                                                             op=ALU.mult),
                            reads=[("s8", half)] + [("opre", half, hh) for hh in range(8)],
                            writes=[("opre", half, hh) for hh in range(8)]))
                        pcs.append(lambda half=half, t=t: op(
                            "dve", lambda e: e.tensor_tensor(out=hb[:, half, :], in0=opre[:, half, :, :].rearrange("p h d -> p (h d)"),
                                                             in1=gate[:, t, :], op=ALU.mult),
                            reads=[("opre", half, hh) for hh in range(8)] + [("gate", t, 0), ("gate", t, 1)] + OV,
                            writes=[("hb", half)]))
                    else:
                        pcs.append(lambda half=half, t=t: op(
                            "dve", lambda e: e.tensor_tensor(out=hb[:, half, :], in0=opre[:, half, :, :].rearrange("p h d -> p (h d)"),
                                                             in1=gate[:, t, :], op=ALU.mult),
                            reads=[("opre", half, hh) for hh in range(8)] + [("gate", t, 0), ("gate", t, 1)] + OV,
                            writes=[("hb", half)]))
                return pcs

            def tail_pieces(qb):
                pcs = []
                for half in range(2):
                    t = 2 * qb + half
                    for g in range(2):
                        pcs.append((lambda half=half, g=g, t=t: transposes_to(
                            lambda ii, half=half, g=g: hb[:, half, (g * 4 + ii) * 128:(g * 4 + ii + 1) * 128], 4, [("hb", half)],
                            lambda half=half, g=g: ogT[:, g * 4:(g + 1) * 4, half * 128:(half + 1) * 128], [("stage", g)]), 1))
                    for nb in range(2):
                        pcs.append((lambda half=half, nb=nb, t=t: wout_piece(t, nb, 0, half), 2))
                        pcs.append((lambda half=half, nb=nb, t=t: wout_piece(t, nb, 1, half), 3))
                return pcs

            def wout_piece(t, nb, part, half):
                for c in range(4 * part, 4 * part + 4):
                    op("pe", lambda e, nb=nb, c=c, half=half: e.matmul(BK[6][:], lhsT=ogT[:, c, half * 128:(half + 1) * 128], rhs=Wr[so[nb]][:, c, :],
                                                                     start=(c == 0), stop=(c == 7)),
                       reads=[("stage", 0), ("stage", 1), ("w", so[nb])], writes=[("bk", 6)])
                if part == 0:
                    return
                op("dve", lambda e, nb=nb: e.tensor_copy(out=scrA[:, nb * 512:(nb + 1) * 512], in_=BK[6][:]),
                   reads=[("bk", 6)], writes=[("scrA", nb)])
                op("dve", lambda e, nb=nb: e.scalar_tensor_tensor(
                    out=q_r[:, 0, :], in0=scrA[:, nb * 512:(nb + 1) * 512], scalar=1.0, in1=scrA[:, nb * 512:(nb + 1) * 512],
                    op0=ALU.mult, op1=ALU.mult, accum_out=stat[:, 36 + nb:37 + nb]),
                   reads=[("scrA", nb)], writes=[("q_r", 0), ("ssy", nb)])
                if nb == 0:
                    return
                op("pool", lambda e: e.tensor_tensor(out=stat[:, 38:39], in0=stat[:, 36:37], in1=stat[:, 37:38], op=ALU.add),
                   reads=[("ssy", 0), ("ssy", 1)], writes=["s38"])
                op("pool", lambda e: e.tensor_scalar(out=stat[:, 38:39], in0=stat[:, 38:39], scalar1=1.0 / D, scalar2=EPS, op0=ALU.mult, op1=ALU.add),
                   reads=["s38"], writes=["s38"])
                op("pool", lambda e: e.tensor_tensor(out=stat[:, 39:40], in0=stat[:, 38:39], in1=cm05[:, 0:1], op=ALU.pow),
                   reads=["s38", "cm05"], writes=["rsy"])
                op("dve", lambda e: e.scalar_tensor_tensor(out=scrA[:], in0=scrA[:], scalar=stat[:, 39:40], in1=modv[:, 2 * D:3 * D],
                                                           op0=ALU.mult, op1=ALU.mult),
                   reads=[("scrA", 0), ("scrA", 1), "rsy", "mG"], writes=[("scrA", 0), ("scrA", 1)])
                op("dve", lambda e, t=t: e.tensor_tensor(out=xs[:, t, :], in0=xs[:, t, :], in1=scrA[:], op=ALU.add),
                   reads=[("scrA", 0), ("scrA", 1), ("x", t)], writes=[("x", t)])
                if last:
                    op("sp", lambda e, t=t: e.dma_start(out=y_d[t * 128:(t + 1) * 128, :], in_=xs[:, t, :]),
                       reads=[("x", t)], writes=[dram_res()], dma=True)

            nsteps = len(steps)
            grp_open = [False]

            def run_pending(step):
                items = pending.pop(step, [])
                deferred = []
                for fn, mode in items:
                    if mode in (1, 2) and grp_open[0]:
                        deferred.append((fn, mode))
                        continue
                    fn()
                    if mode == 2:
                        grp_open[0] = True
                    elif mode == 3:
                        grp_open[0] = False
                if deferred:
                    pending[step + 1] = deferred + pending.get(step + 1, [])
            if i + 1 < NL:
                apcs = ada_pieces(i + 1, Gn, ["Gn"], land, ["land"], land, ["land"], 6, OV, split=True)
                sp_ = 16
                pending.setdefault(5, []).append(apcs[0])
                for b_ in range(6):
                    pending.setdefault(20 + 20 * b_, []).append(apcs[1 + 2 * b_])
                    pending.setdefault(21 + 20 * b_, []).append(apcs[2 + 2 * b_])
                pending.setdefault(135, []).append(apcs[13])
                pending.setdefault(150, []).append(apcs[14])
                pending.setdefault(150 + sp_ // 2, []).append((lambda li_=i: cache_loads(li_ + 1, landb, ["land"]), 0))
            for i in range(nsteps + LA):
                if DBG is not None and DBG >= 100 and i >= DBG - 100:
                    raise _Stop()
                jx = i - LA
                if i < nsteps:
                    if i == 0:
                        fill_qz(0)
                        fill_qz(1)
                    elif steps[i][1] == 0:
                        fill_qz(steps[i][0] + 1)
                    S_step(i)
                if jx >= 0:
                    PV_step(jx)
                    u, kt = steps[jx]
                    if kt == NKT - 1:
                        unit_epilogue(u)
                        qb = units[u][0]
                        if u + 1 == len(units) or units[u + 1][0] != qb:
                            p1 = tail_stage1(qb)
                            for k_, pc in enumerate(p1):
                                pending.setdefault(jx + 3 + 2 * k_, []).append((pc, 0))
                            for k_, pc in enumerate(tail_pieces(qb)):
                                pending.setdefault(jx + 24 + k_ + k_ // 2, []).append(pc)
                    run_pending(jx)
            while pending:
                run_pending(min(pending))
            w_done(so[0])
            w_done(so[1])

        try:
            for li in range(NL):
                layer(li)
        except _Stop:
            pass

        op("sp", None, reads=list(dram_list))
        P.emit()
    return nc


_NC_CACHE = {}


def _rope_tables():
    rows = T // 64
    row = np.repeat(np.arange(rows), 64).astype(np.float32)
    col = np.tile(np.arange(64), rows).astype(np.float32)
    inv = (np.float32(10000.0) ** (-np.arange(16, dtype=np.float32) / np.float32(16))).astype(np.float32)
    ang = np.concatenate([row[:, None] * inv, col[:, None] * inv], axis=-1).astype(np.float32)
    return np.cos(ang).astype(np.float32), np.sin(ang).astype(np.float32)


def kernel(x_prompt, x_sample, cache_diff_k, cache_diff_v, cache_mla_ckv, cache_mla_kpe,
           c, c_ctx, w_ada, b_ada, g_pre, g_post, w_out,
           da_w_in, da_lam_q1, da_lam_k1, da_lam_q2, da_lam_k2, da_g_sub,
           mla_w_in, mla_g_qa, mla_w_qb, mla_g_kva, mla_w_kvb, _NL=4):
    f = lambda a: np.ascontiguousarray(np.asarray(a, dtype=np.float32))
    x_prompt, x_sample = f(x_prompt), f(x_sample)
    if _NL not in _NC_CACHE:
        _NC_CACHE[_NL] = build_nc(_NL)
    nc = _NC_CACHE[_NL]
    cosv, sinv = _rope_tables()
    ones = np.ones_like(cosv)
    zeros = np.zeros_like(sinv)
    pq = np.concatenate([np.arange(h * 192, h * 192 + 128) for h in range(8)] +
                        [np.arange(h * 192 + 128, (h + 1) * 192) for h in range(8)])
    pkv = np.concatenate([np.arange(h * 256, h * 256 + 128) for h in range(8)] +
                         [np.arange(h * 256 + 128, (h + 1) * 256) for h in range(8)])
    shared = {
        "w_ada": f(w_ada), "b_ada": f(b_ada), "g_pre": f(g_pre), "g_post": f(g_post), "w_out": f(w_out),
        "da_w_in": f(da_w_in),
        "lamv": f(np.stack([np.asarray(da_lam_q1), np.asarray(da_lam_k1), np.asarray(da_lam_q2), np.asarray(da_lam_k2)], axis=1)),
        "da_g_sub": f(da_g_sub), "mla_w_in": f(mla_w_in), "mla_g_qa": f(mla_g_qa),
        "mla_w_qb": f(np.asarray(mla_w_qb)[:, :, pq]), "mla_g_kva": f(mla_g_kva),
        "mla_w_kvb": f(np.asarray(mla_w_kvb)[:, :, pkv]),
    }
    mb_lat = np.zeros((40,), np.float32)
    mb_ctx = np.full((4, 10), NEG, np.float32)
    for qb in range(4):
        mb_ctx[qb, 2 * qb:2 * qb + 2] = 0.0
    mb_ctx = mb_ctx.reshape(40)
    cdk = f(cache_diff_k).reshape(4, 2, 256, 1024)
    cdv = f(cache_diff_v).reshape(4, 2, 256, 1024)
    cckv = f(cache_mla_ckv)
    ckpe = f(cache_mla_kpe)
    c = f(c)
    c_ctx = f(c_ctx)
    in_maps = []
    for core in range(8):
        m = dict(shared)
        if core < 4:
            b = core
            m.update(x=x_sample[b], cond=c[b], cosT=cosv, sinT=sinv, mb=mb_lat,
                     cdk=cdk[b], cdv=cdv[b], cckv=cckv[b], ckpe=ckpe[b])
        else:
            g = core - 4
            m.update(x=np.ascontiguousarray(x_prompt[4 * g:4 * g + 4].reshape(T, D)), cond=c_ctx, cosT=ones, sinT=zeros, mb=mb_ctx,
                     cdk=np.zeros((2, 256, 1024), np.float32), cdv=np.zeros((2, 256, 1024), np.float32),
                     cckv=np.zeros((2, 256, 256), np.float32), ckpe=np.zeros((2, 256, 64), np.float32))
        in_maps.append(m)
    res = run_bass_kernel_spmd(nc, in_maps, core_ids=list(range(8)))
    r = res.results
    y_sample = np.stack([r[b]["y"] for b in range(4)], axis=0).astype(np.float32)
    y_prompt = np.concatenate([r[4 + g]["y"].reshape(4, 256, D) for g in range(4)], axis=0).astype(np.float32)

    def gather(name, last):
        outs = []
        for g in range(4):
            a = r[4 + g][name].reshape(2, 4, 256, last)
            outs.append(np.transpose(a, (1, 0, 2, 3)))
        return np.concatenate(outs, axis=0).astype(np.float32)

    sdk = gather("kst", 1024).reshape(16, 2, 256, 8, 2, 64)
    sdv = gather("vst", 1024).reshape(16, 2, 256, 8, 128)
    sckv = gather("ckvst", 256)
    skpe = gather("kpest", 64)
    return (y_prompt, y_sample, sdk, sdv, sckv, skpe)
```

```python
import contextlib
import math
import os
import numpy as np
import concourse.bass as bass
import concourse.mybir as mybir
from concourse.bass_utils import run_bass_kernel_spmd

F32 = mybir.dt.float32
BF16 = mybir.dt.bfloat16
AF = mybir.ActivationFunctionType
ALU = mybir.AluOpType

ENGS = ("pe", "act", "dve", "pool", "sp")
N_DMA_SEMS = 94
N_SW_SEMS = 24

D = 1024
T = 1024
NT = 8
NK = 1280
NKT = 10
EPS = 1e-6
NS = 3
NEG = -30000.0


class Prog:
    def __init__(self, nc, same_engine_sync=True):
        self.nc = nc
        self.ops = []
        self.last_writer = {}
        self.readers = {}
        self.same_engine_sync = same_engine_sync
        self.n_dma_grp = [0, 0]
        self.dma_slot_last = {}

    def op(self, eng, fn, reads=(), writes=(), dma=False):
        idx = len(self.ops)
        deps = set()
        lock = [("bklock", r[1]) for r in reads if isinstance(r, tuple) and r[0] == "bk"]
        if lock:
            writes = list(writes) + lock
        for r in reads:
            w = self.last_writer.get(r)
            if w is not None:
                deps.add(w)
        for r in writes:
            w = self.last_writer.get(r)
            if w is not None:
                deps.add(w)
            for rd in self.readers.get(r, ()):
                deps.add(rd)
        o = dict(eng=eng, fn=fn, deps=deps, dma=dma, idx=idx)
        if dma:
            grp = 0 if eng == "pool" else 1
            base, size = ((0, N_SW_SEMS), (N_SW_SEMS, N_DMA_SEMS - N_SW_SEMS))[grp]
            n = self.n_dma_grp[grp]
            slot = base + n % size
            o["slot"] = slot
            o["val"] = 16 * (n // size + 1)
            prev = self.dma_slot_last.get(slot)
            if prev is not None:
                deps.add(prev)
            self.dma_slot_last[slot] = idx
            self.n_dma_grp[grp] += 1
        deps.discard(idx)
        best = {}
        keep = set()
        for d_ in deps:
            p = self.ops[d_]
            if p["dma"]:
                keep.add(d_)
            else:
                if p["eng"] not in best or best[p["eng"]] < d_:
                    best[p["eng"]] = d_
        keep.update(best.values())
        o["deps"] = keep
        self.ops.append(o)
        for r in reads:
            self.readers.setdefault(r, []).append(idx)
        for r in writes:
            self.last_writer[r] = idx
            self.readers[r] = []
        return idx

    def emit(self):
        nc = self.nc
        ops = self.ops
        needs_signal = set()
        for o in ops:
            for d in o["deps"]:
                p = ops[d]
                if p["dma"]:
                    continue
                if p["eng"] == o["eng"] and not o["dma"]:
                    if p["eng"] == "pe" or not self.same_engine_sync:
                        continue
                needs_signal.add(d)
        ordinal = {}
        cnt = {e: 0 for e in ENGS}
        for o in ops:
            if o["idx"] in needs_signal:
                cnt[o["eng"]] += 1
                ordinal[o["idx"]] = cnt[o["eng"]]
        self.sig_counts = cnt
        with contextlib.ExitStack() as st:
            esem = {e: st.enter_context(nc.semaphore("s_" + e)) for e in ENGS}
            dsem = [st.enter_context(nc.semaphore("d_%d" % i)) for i in range(N_DMA_SEMS)]
            block = st.enter_context(nc.Block())
            per_eng = {e: [o for o in ops if o["eng"] == e] for e in ENGS}

            def run(e, engobj):
                waited = {}
                for o in per_eng[e]:
                    for d in sorted(o["deps"]):
                        p = ops[d]
                        if p["dma"]:
                            key = ("d", p["slot"])
                            val = p["val"]
                            sem = dsem[p["slot"]]
                        else:
                            if d not in ordinal:
                                continue
                            key = ("e", p["eng"])
                            val = ordinal[d]
                            sem = esem[p["eng"]]
                        if waited.get(key, 0) >= val:
                            continue
                        waited[key] = val
                        engobj.wait_ge(sem, val)
                    if o["fn"] is None:
                        continue
                    ins = o["fn"](engobj)
                    if o["dma"]:
                        ins.then_inc(dsem[o["slot"]], 16)
                    elif o["idx"] in ordinal:
                        ins.then_inc(esem[e], 1)

            block.tensor(lambda eng: run("pe", eng))
            block.scalar(lambda eng: run("act", eng))
            block.vector(lambda eng: run("dve", eng))
            block.gpsimd(lambda eng: run("pool", eng))
            block.sync(lambda eng: run("sp", eng))


def lam_init_of(i):
    return 0.8 - 0.6 * math.exp(-0.3 * i)


class _Stop(Exception):
    pass


def build_nc(NL=4, DBG=None):
    nc = bass.Bass("TRN2", target_bir_lowering=False)

    def din(name, shape):
        return nc.dram_tensor(name, list(shape), F32, kind="ExternalInput").ap()

    def dout(name, shape):
        return nc.dram_tensor(name, list(shape), F32, kind="ExternalOutput").ap()

    x_d = din("x", [T, D])
    cond_d = din("cond", [D])
    cos_d = din("cosT", [T, 32])
    sin_d = din("sinT", [T, 32])
    mb_d = din("mb", [40])
    cdk_d = din("cdk", [2, 256, 1024])
    cdv_d = din("cdv", [2, 256, 1024])
    cckv_d = din("cckv", [2, 256, 256])
    ckpe_d = din("ckpe", [2, 256, 64])
    w_ada_d = din("w_ada", [4, D, 3 * D])
    b_ada_d = din("b_ada", [4, 3 * D])
    g_pre_d = din("g_pre", [4, D])
    g_post_d = din("g_post", [4, D])
    w_out_d = din("w_out", [4, D, D])
    da_w_in_d = din("da_w_in", [2, D, 4096])
    lamv_d = din("lamv", [2, 4, 64])
    da_g_sub_d = din("da_g_sub", [2, 128])
    mla_w_in_d = din("mla_w_in", [2, D, 1728])
    mla_g_qa_d = din("mla_g_qa", [2, 384])
    mla_w_qb_d = din("mla_w_qb", [2, 384, 1536])
    mla_g_kva_d = din("mla_g_kva", [2, 256])
    mla_w_kvb_d = din("mla_w_kvb", [2, 256, 2048])

    y_d = dout("y", [T, D])
    kst_d = dout("kst", [2, T, 1024])
    vst_d = dout("vst", [2, T, 1024])
    ckvst_d = dout("ckvst", [2, T, 256])
    kpest_d = dout("kpest", [2, T, 64])

    st = contextlib.ExitStack()
    with st:
        def sb(name, shape, dt):
            return st.enter_context(nc.sbuf_tensor(name, list(shape), dt))

        xs = sb("xs", [128, NT, D], F32)
        qT = sb("qT", [128, 12, T], BF16)
        kT = sb("kT", [128, 8, NK], BF16)
        Vp = sb("Vp", [128, NKT, 8, 130], BF16)
        gate = sb("gate", [128, NT, D], BF16)
        modv = sb("modv", [128, 3 * D], F32)
        Wr = [sb("W%d" % i, [128, 8, 512], BF16) for i in range(NS)]
        ckvT = sb("ckvT", [128, 2, NK], BF16)
        kpeT = sb("kpeT", [128, NK], BF16)
        scrA = sb("scrA", [128, D], F32)
        hT = sb("hT", [128, 8, T], BF16)
        hb = sb("hb", [128, 2, D], BF16)
        q_r = sb("q_r", [128, 2, 512], BF16)
        stage = sb("stage", [128, 2, 512], F32)
        ident = sb("ident", [128, 128], BF16)
        cos2 = sb("cos2_s", [128, NT, 64], F32)
        nsin = sb("nsin_s", [128, NT, 32], F32)
        sinT = sb("sinT_s", [128, NT, 32], F32)
        mb = sb("mb_s", [128, 40], F32)
        S_rep = sb("S_rep", [128, 8, 128], BF16)
        lsm = sb("lsm", [128, 16], F32)
        neglam = sb("neglam", [128, 2], F32)
        gs_bc = sb("gs_bc", [128, 2, 128], F32)
        gqa_bc = sb("gqa_bc", [128, 384], F32)
        gkva_bc = sb("gkva_bc", [128, 256], F32)
        stat = sb("stat", [128, 80], F32)
        cm05 = sb("cm05", [128, 8], F32)
        kpe_st = sb("kpe_st", [128, 2, 64], F32)
        kpe_b = sb("kpe_b", [128, 128], BF16)
        ovb = sb("ovb", [128, 1280], F32)
        PT = ovb[:, 0:768].bitcast(BF16).rearrange("p (a n) -> p a n", a=3)
        t0 = ovb[:, 768:1280].rearrange("p (s h d) -> p s h d", s=2, h=2)
        rt = ovb[:, 0:1024].rearrange("p (a h d) -> p a h d", a=2, h=8)
        qzA = sb("qzA", [128, 2, 2, 256], BF16)
        qzB = sb("qzB", [128, 2, 2, 256], BF16)

        hTf = hT[:].rearrange("p c t -> p (c t)").bitcast(F32)
        opre = hTf[:, 0:2048].rearrange("p (a h d) -> p a h d", a=2, h=8)
        Gn = hTf[:, 2048:3072]
        land = hTf[:, 3072:4096]
        stage_flat = stage[:].rearrange("p a n -> p (a n)")
        ogT = stage_flat.bitcast(BF16).rearrange("p (c n) -> p c n", c=8)
        lamt = stage_flat[:, 0:512].rearrange("p (a b c) -> p a b c", a=2, b=4)
        identf = t0[:, 1, 0, :]
        io = t0[:, 1, 1, :].bitcast(mybir.dt.int32)
        qaT = modv[:, 0:2048].bitcast(BF16)[:, 0:3 * T].rearrange("p (c t) -> p c t", c=3)

        BK = [st.enter_context(nc.psum_tensor("bk%d" % i, [128, 512], F32)) for i in range(8)]
        TBk = {k: BK[k][:].bitcast(BF16)[:, 0:512].rearrange("p (c n) -> p c n", c=4) for k in (5, 6)}
        tb_banks = [5, 6]

        P = Prog(nc)
        op = P.op
        dram_list = []

        def dram_res():
            r = ("dram", len(dram_list))
            dram_list.append(r)
            return r

        wq = []
        wstate = dict(issued=0, taken=0)
        wfree = list(range(NS))
        slot_of = {}

        def w_issue():
            while wstate["issued"] < len(wq) and wfree:
                n = wstate["issued"]
                src, kc, ncols = wq[n]
                s = wfree.pop(0)
                slot_of[n] = s
                op("pool", lambda e, s=s, src=src, kc=kc, ncols=ncols: e.dma_start(out=Wr[s][:, 0:kc, 0:ncols], in_=src),
                   writes=[("w", s)], dma=True)
                wstate["issued"] += 1

        def w_take():
            n = wstate["taken"]
            assert n < wstate["issued"], "weight not issued"
            wstate["taken"] += 1
            return slot_of[n]

        def w_done(s):
            wfree.append(s)
            w_issue()

        def wsrc(ap2d, kc):
            return ap2d.rearrange("(c p) n -> p c n", p=128)

        for i in range(NL):
            j = i // 2
            for blk in (range(3) if i == 0 else range(6)):
                wq.append((wsrc(w_ada_d[i][:, blk * 512:(blk + 1) * 512], 8), 8, 512))
            if i % 2 == 0:
                for blk in range(8):
                    wq.append((wsrc(da_w_in_d[j][:, blk * 512:(blk + 1) * 512], 8), 8, 512))
            else:
                wq.append((wsrc(mla_w_in_d[j][:, 0:384], 8), 8, 384))
                wq.append((wsrc(mla_w_in_d[j][:, 384:704], 8), 8, 320))
                for blk in range(2):
                    wq.append((wsrc(mla_w_in_d[j][:, 704 + blk * 512:704 + (blk + 1) * 512], 8), 8, 512))
                for blk in range(3):
                    wq.append((wsrc(mla_w_qb_d[j][:, blk * 512:(blk + 1) * 512], 3), 3, 512))
                for blk in range(4):
                    wq.append((wsrc(mla_w_kvb_d[j][:, blk * 512:(blk + 1) * 512], 2), 2, 512))
            for blk in range(2):
                wq.append((wsrc(w_out_d[i][:, blk * 512:(blk + 1) * 512], 8), 8, 512))

        for t in range(NT):
            op("sp", lambda e, t=t: e.dma_start(out=xs[:, t, :], in_=x_d[t * 128:(t + 1) * 128, :]),
               writes=[("x", t)], dma=True)
        for hh_ in range(2):
            op("sp", lambda e, hh_=hh_: e.dma_start(out=cos2[:, :, hh_ * 32:(hh_ + 1) * 32], in_=cos_d.rearrange("(t p) f -> p t f", p=128)),
               writes=[("cos", hh_)], dma=True)
        op("sp", lambda e: e.dma_start(out=sinT[:], in_=sin_d.rearrange("(t p) f -> p t f", p=128)), writes=["sin"], dma=True)
        op("sp", lambda e: e.dma_start(out=mb[:], in_=mb_d.partition_broadcast(128)), writes=["mb"], dma=True)
        op("sp", lambda e: e.dma_start(out=scrA[:], in_=cond_d.partition_broadcast(128)), writes=[("scrA", 0), ("scrA", 1)], dma=True)
        op("sp", lambda e: e.dma_start(out=stage_flat[:, 0:512],
                                       in_=lamv_d.rearrange("a b c -> (a b c)").partition_broadcast(128)), writes=["lamt", ("stage", 0)], dma=True)
        op("sp", lambda e: e.dma_start(out=gs_bc[:].rearrange("p a b -> p (a b)"),
                                       in_=da_g_sub_d.rearrange("a b -> (a b)").partition_broadcast(128)), writes=["gs"], dma=True)
        w_issue()
        qTflat = qT[:].rearrange("p c t -> p (c t)")
        ada0_direct = {}
        for k_ in range(3):
            sl_ = qTflat[:, k_ * 4096:(k_ + 1) * 4096].rearrange("p (c n) -> p c n", c=8)
            ada0_direct[3 + k_] = (sl_, [("qTslot", k_)])
            op("pool", lambda e, sl_=sl_, k_=k_: e.dma_start(out=sl_, in_=wsrc(w_ada_d[0][:, (3 + k_) * 512:(4 + k_) * 512], 8)),
               writes=[("qTslot", k_)], dma=True)

        op("pool", lambda e: e.memset(cm05[:], -0.5), writes=["cm05"])
        op("pool", lambda e: e.memset(qzA[:], 0.0), writes=[("qz", 0, a_, b_) for a_ in range(2) for b_ in range(2)])
        op("pool", lambda e: e.memset(qzB[:], 0.0), writes=[("qz", 1, a_, b_) for a_ in range(2) for b_ in range(2)])
        op("dve", lambda e: e.tensor_scalar(out=nsin[:], in0=sinT[:], scalar1=-1.0, scalar2=None, op0=ALU.mult), reads=["sin"], writes=["nsin"])
        op("pool", lambda e: e.iota(io, [[1, 128]], base=0, channel_multiplier=-1), writes=["io"])
        op("dve", lambda e: e.tensor_scalar(out=identf, in0=io, scalar1=0, scalar2=None, op0=ALU.is_equal),
           reads=["io"], writes=["identf"])
        op("dve", lambda e: e.tensor_copy(out=ident[:], in_=identf), reads=["identf"], writes=["ident"])
        op("pool", lambda e: e.memset(Vp[:, :, :, 128:130], 1.0), writes=["Vp_ones"])

        for j in range(2):
            for k in range(2):
                op("dve", lambda e, j=j, k=k: e.scalar_tensor_tensor(
                    out=q_r[:, 0, 0:64], in0=lamt[:, j, 2 * k, :], scalar=1.0, in1=lamt[:, j, 2 * k + 1, :],
                    op0=ALU.mult, op1=ALU.mult, accum_out=lsm[:, 2 * j + k:2 * j + k + 1]),
                   reads=["lamt"], writes=[("lsm", j, k), ("q_r", 0)])
        op("act", lambda e: e.activation(out=lsm[:, 4:8], in_=lsm[:, 0:4], func=AF.Exp),
           reads=[("lsm", 0, 0), ("lsm", 0, 1), ("lsm", 1, 0), ("lsm", 1, 1)], writes=["lsm_e"])
        for j in range(2):
            li = lam_init_of(2 * j)
            op("dve", lambda e, j=j: e.tensor_tensor(out=lsm[:, 8 + j:9 + j], in0=lsm[:, 4 + 2 * j:5 + 2 * j],
                                                     in1=lsm[:, 5 + 2 * j:6 + 2 * j], op=ALU.subtract),
               reads=["lsm_e"], writes=[("lsm_d", j)])
            op("dve", lambda e, j=j, li=li: e.tensor_scalar(out=neglam[:, j:j + 1], in0=lsm[:, 8 + j:9 + j],
                                                          scalar1=li, scalar2=-1.0, op0=ALU.add, op1=ALU.mult),
               reads=[("lsm_d", j)], writes=[("neglam", j)])
            op("dve", lambda e, j=j, li=li: e.tensor_scalar(out=gs_bc[:, j, :], in0=gs_bc[:, j, :], scalar1=1.0 - li,
                                                          scalar2=None, op0=ALU.mult),
               reads=["gs"], writes=[("gsj", j)])

        op("act", lambda e: e.activation(out=hb[:, 0, :], in_=scrA[:], func=AF.Silu), reads=[("scrA", 0), ("scrA", 1)], writes=[("hb", 0)])
        for a in range(2):
            for c in range(4):
                op("pe", lambda e, a=a, c=c: e.transpose(out=TBk[5 + a][:, c, :], in_=hb[:, 0, (a * 4 + c) * 128:(a * 4 + c + 1) * 128],
                                                        identity=ident[:]),
                   reads=[("hb", 0), "ident"], writes=[("bk", 5 + a)])
            op("dve", lambda e, a=a: e.tensor_copy(out=S_rep[:, a * 4:(a + 1) * 4, :], in_=TBk[5 + a][:, :, :]),
               reads=[("bk", 5 + a)], writes=["S_rep"])

        OV = ["ovl"]

        bank_rr = dict(a=0, tb=0, st=0, qr=0)

        def next_bank6():
            b = bank_rr["a"] % 5
            bank_rr["a"] += 1
            return b

        def next_tb():
            a = tb_banks[bank_rr["tb"] % len(tb_banks)]
            bank_rr["tb"] += 1
            return a

        SR = [(stage[:, 0, :], ("stage", 0)), (stage[:, 1, :], ("stage", 1)),
              (scrA[:, 0:512], ("scrA", 0)), (scrA[:, 512:1024], ("scrA", 1))]

        def next_stage():
            a = bank_rr["st"] % 4
            bank_rr["st"] += 1
            return a

        def next_qr():
            a = bank_rr["qr"] % 2
            bank_rr["qr"] += 1
            return a

        def transposes_to(src_ap_fn, nblk, src_res, dst_fn, dst_res, copy_eng="dve"):
            a = next_tb()
            for i in range(nblk):
                op("pe", lambda e, a=a, i=i: e.transpose(out=TBk[a][:, i, :], in_=src_ap_fn(i), identity=ident[:]),
                   reads=list(src_res) + ["ident"], writes=[("bk", a)])
            if copy_eng == "act":
                op("act", lambda e, a=a: e.activation(out=dst_fn(), in_=TBk[a][:, 0:nblk, :], func=AF.Identity),
                   reads=[("bk", a)], writes=list(dst_res))
            else:
                op("dve", lambda e, a=a: e.tensor_copy(out=dst_fn(), in_=TBk[a][:, 0:nblk, :]),
                   reads=[("bk", a)], writes=list(dst_res))

        def rope_block(src3, nh, t, dst3, src_res, dst_res, eng_mul="dve", eng_add="dve"):
            cb = cos2[:, t, :].unsqueeze(1).broadcast_to([128, nh, 64])
            sn = sinT[:, t, :].unsqueeze(1).broadcast_to([128, nh, 32])
            ns = nsin[:, t, :].unsqueeze(1).broadcast_to([128, nh, 32])
            ra = rt[:, 0, 0:nh, :]
            rb = rt[:, 1, 0:nh, :]
            op(eng_mul, lambda e: e.tensor_tensor(out=ra, in0=src3, in1=cb, op=ALU.mult),
               reads=list(src_res) + [("cos", 0), ("cos", 1)] + OV, writes=[("rt", 0)])
            op(eng_mul, lambda e: e.tensor_tensor(out=rb[:, :, 0:32], in0=src3[:, :, 32:64], in1=ns, op=ALU.mult),
               reads=list(src_res) + ["nsin"] + OV, writes=[("rt", 1)])
            op(eng_mul, lambda e: e.tensor_tensor(out=rb[:, :, 32:64], in0=src3[:, :, 0:32], in1=sn, op=ALU.mult),
               reads=list(src_res) + ["sin"] + OV, writes=[("rt", 2)])
            op(eng_add, lambda e: e.tensor_tensor(out=dst3, in0=ra, in1=rb, op=ALU.add),
               reads=[("rt", 0), ("rt", 1), ("rt", 2)] + OV, writes=list(dst_res))

        def barrier(tag):
            op("pool", lambda e: e.memset(stat[:, 63:64], 0.0), reads=[], writes=["ovl"] + [("hT", t_) for t_ in range(NT)])

        def chk(level):
            if DBG is not None and level >= DBG:
                raise _Stop()

        landb = land.bitcast(BF16).rearrange("p (a n) -> p a n", a=2)

        def cache_loads(i, stg, sres):
            j = i // 2
            if i % 2 == 0:
                op("pool", lambda e: e.dma_start(out=stg, in_=cdk_d[j].rearrange("(a p) n -> p a n", p=128)),
                   reads=OV, writes=list(sres), dma=True)
            else:
                op("pool", lambda e: e.dma_start(out=stg[:, 0, 0:512].rearrange("p (a n) -> p a n", a=2),
                                                 in_=cckv_d[j].rearrange("(a p) n -> p a n", p=128)),
                   reads=OV, writes=list(sres), dma=True)
                for dup in range(2):
                    op("pool", lambda e, dup=dup: e.dma_start(
                        out=stg[:, 1, 0:256].rearrange("p (a n) -> p a n", a=2)[:, :, dup * 64:(dup + 1) * 64],
                        in_=ckpe_d[j].rearrange("(a p) n -> p a n", p=128)),
                       reads=OV, writes=list(sres), dma=True)

        def cache_transposes(i, stg, sres):
            rr = list(sres) + OV
            if i % 2 == 0:
                for a in range(2):
                    for g in range(2):
                        transposes_to(lambda ii, a=a, g=g: stg[:, a, (g * 4 + ii) * 128:(g * 4 + ii + 1) * 128], 4, rr,
                                      lambda a=a, g=g: kT[:, g * 4:(g + 1) * 4, T + a * 128:T + (a + 1) * 128], [("kT", 8 + a)])
            else:
                for a in range(2):
                    transposes_to(lambda ii, a=a: stg[:, 0, a * 256 + ii * 128:a * 256 + (ii + 1) * 128], 2, rr,
                                  lambda a=a: ckvT[:, 0:2, T + a * 128:T + (a + 1) * 128], [("ckvT", 8 + a)])
                for a in range(2):
                    transposes_to(lambda ii, a=a: stg[:, 1, a * 128:(a + 1) * 128], 1, rr,
                                  lambda a=a: kpeT[:, T + a * 128:T + (a + 1) * 128].unsqueeze(1), [("kpeT", 8 + a)])

        def cache_v(i):
            j = i // 2
            if i % 2 == 0:
                for a in range(2):
                    op("pool", lambda e, a=a: e.dma_start(out=Vp[:, 8 + a, :, 0:128],
                                                          in_=cdv_d[j][a * 128:(a + 1) * 128, :].rearrange("p (h d) -> p h d", h=8)),
                       writes=[("Vp", 8 + a)], dma=True)

        def ada_pieces(i, dG, dG_res, lpre, lpre_res, lpost, lpost_res, bank, extra, split=False, direct=None):
            pcs = []

            def p_bias():
                op("sp", lambda e: e.dma_start(out=modv[:, 0:2 * D], in_=b_ada_d[i][0:2 * D].partition_broadcast(128)),
                   reads=list(extra), writes=["mBA"], dma=True)
                op("sp", lambda e: e.dma_start(out=dG, in_=b_ada_d[i][2 * D:3 * D].partition_broadcast(128)),
                   reads=list(extra), writes=list(dG_res), dma=True)
            pcs.append((p_bias, 0) if split else p_bias)

            slot_mem = {}

            def p_blk(blk, part=None):
                is_direct = direct is not None and blk in direct
                if is_direct:
                    wap, wres = direct[blk]
                else:
                    if part in (None, 0):
                        slot_mem[blk] = w_take()
                    s = slot_mem[blk]
                    wap, wres = Wr[s], [("w", s)]
                crange = range(8) if part is None else range(4 * part, 4 * part + 4)
                for c in crange:
                    op("pe", lambda e, wap=wap, c=c: e.matmul(BK[bank][:], lhsT=S_rep[:, c, :], rhs=wap[:, c, :], start=(c == 0), stop=(c == 7)),
                       reads=["S_rep"] + list(wres), writes=[("bk", bank)])
                if part == 0:
                    return
                if blk < 4:
                    dst, dres = modv[:, blk * 512:(blk + 1) * 512], ["mBA"]
                else:
                    dst, dres = dG[:, (blk - 4) * 512:(blk - 3) * 512], list(dG_res)
                op("dve", lambda e: e.tensor_tensor(out=dst, in0=BK[bank][:], in1=dst, op=ALU.add),
                   reads=[("bk", bank)] + dres + list(extra), writes=dres)
                if not is_direct:
                    w_done(slot_mem[blk])
            for blk in range(6):
                if split:
                    pcs.append((lambda blk=blk: p_blk(blk, 0), 2))
                    pcs.append((lambda blk=blk: p_blk(blk, 1), 3))
                else:
                    pcs.append(lambda blk=blk: p_blk(blk))

            def p_pre():
                op("sp", lambda e: e.dma_start(out=lpre, in_=g_pre_d[i].partition_broadcast(128)), reads=list(extra), writes=list(lpre_res), dma=True)
                op("dve", lambda e: e.scalar_tensor_tensor(out=modv[:, D:2 * D], in0=modv[:, D:2 * D], scalar=1.0, in1=lpre,
                                                           op0=ALU.add, op1=ALU.mult), reads=["mBA"] + list(lpre_res) + list(extra), writes=["mBA"])
            pcs.append((p_pre, 0) if split else p_pre)

            def p_post():
                op("sp", lambda e: e.dma_start(out=lpost, in_=g_post_d[i].partition_broadcast(128)), reads=list(extra), writes=list(lpost_res), dma=True)
                op("dve", lambda e: e.tensor_tensor(out=dG, in0=dG, in1=lpost, op=ALU.mult),
                   reads=list(dG_res) + list(lpost_res) + list(extra), writes=list(dG_res))
            pcs.append((p_post, 0) if split else p_post)
            return pcs

        def layer(i):
            j = i // 2
            chk(0)
            kind = i % 2
            last = (i == NL - 1)

            for t in range(NT):
                op("act", lambda e, t=t: e.activation(out=hb[:, 1, :], in_=xs[:, t, :], func=AF.Square, accum_out=stat[:, t:t + 1]),
                   reads=[("x", t)], writes=[("hb", 1), ("ssx", t)])
            op("pool", lambda e: e.tensor_scalar(out=stat[:, 8:16], in0=stat[:, 0:8], scalar1=1.0 / D, scalar2=EPS, op0=ALU.mult, op1=ALU.add),
               reads=[("ssx", t) for t in range(NT)], writes=["msx"])
            op("pool", lambda e: e.tensor_tensor(out=stat[:, 16:24], in0=stat[:, 8:16], in1=cm05[:, 0:8], op=ALU.pow),
               reads=["msx", "cm05"], writes=["rsx"])
            if kind == 1:
                op("sp", lambda e, j=j: e.dma_start(out=gqa_bc[:], in_=mla_g_qa_d[j].partition_broadcast(128)), writes=["gqa"], dma=True)
                op("sp", lambda e, j=j: e.dma_start(out=gkva_bc[:], in_=mla_g_kva_d[j].partition_broadcast(128)), writes=["gkva"], dma=True)
            if i == 0:
                for pc in ada_pieces(0, modv[:, 2 * D:3 * D], ["mG"], scrA[:], [("scrA", 0), ("scrA", 1)],
                                     stage_flat, [("stage", 0), ("stage", 1)], 7, [], direct=ada0_direct):
                    pc()
                op("pool", lambda e: e.memset(stat[:, 62:63], 0.0), reads=[("qTslot", k_) for k_ in range(3)],
                   writes=[("qT", t_) for t_ in range(NT)])
            else:
                op("dve", lambda e: e.tensor_copy(out=modv[:, 2 * D:3 * D], in_=Gn), reads=["Gn"] + OV, writes=["mG"])
                tb_banks[:] = [6]
                cache_transposes(i, landb, ["land"])

            chk(1)
            barrier("A")
            tb_banks[:] = [5, 6]
            if i == 0:
                cache_loads(i, hb[:], [("hb", 0), ("hb", 1)])
                cache_transposes(i, hb[:], [("hb", 0), ("hb", 1)])
            cache_v(i)

            chk(2)
            chk(3)

            dq = []
            blk_counter = [0]

            def run_deferred(all_=False):
                while dq and (all_ or dq[0][0] <= blk_counter[0]):
                    dq.pop(0)[1]()

            def proj_block(s, t, ncols, kc=8, lhs=None, lhs_res=None):
                blk_counter[0] += 1
                b = next_bank6()
                for c in range(kc):
                    l = (hT if lhs is None else lhs)
                    op("pe", lambda e, b=b, c=c, l=l: e.matmul(BK[b][:, 0:ncols], lhsT=l[:, c, t * 128:(t + 1) * 128],
                                                              rhs=Wr[s][:, c, 0:ncols], start=(c == 0), stop=(c == kc - 1)),
                       reads=[("w", s)] + ([("hT", t)] if lhs is None else [(lhs_res, t), "mBA"]), writes=[("bk", b)])
                run_deferred()
                return b

            def q_like_evac(b, t, chunk0, rope, dst, dres, src=None, src_res=None):
                qa = next_qr()
                if rope:
                    sap = BK[b][:] if src is None else src
                    sres = [("bk", b)] if src is None else list(src_res)
                    rope_block(sap.rearrange("p (h d) -> p h d", d=64), 8, t,
                               q_r[:, qa, :].rearrange("p (h d) -> p h d", d=64), sres, [("q_r", qa)] + OV)
                else:
                    op("act", lambda e, b=b, qa=qa: e.activation(out=q_r[:, qa, :], in_=BK[b][:], func=AF.Identity),
                       reads=[("bk", b)] + OV, writes=[("q_r", qa)])
                dq.append((blk_counter[0] + 2, lambda qa=qa: transposes_to(
                    lambda ii, qa=qa: q_r[:, qa, ii * 128:(ii + 1) * 128], 4, [("q_r", qa)],
                    lambda: dst[:, chunk0:chunk0 + 4, t * 128:(t + 1) * 128], [(dres, t)], copy_eng="act")))

            def state_out(b, ncols, c0, dst_dram_fn, also=None):
                sa = next_stage()
                op("act", lambda e, b=b, sa=sa: e.activation(out=SR[sa][0][:, 0:ncols], in_=BK[b][:, c0:c0 + ncols], func=AF.Identity),
                   reads=[("bk", b)] + OV, writes=[SR[sa][1]])
                op("sp", lambda e, sa=sa: e.dma_start(out=dst_dram_fn(), in_=SR[sa][0][:, 0:ncols]),
                   reads=[SR[sa][1]], writes=[dram_res()], dma=True)
                return sa

            def do_block(cb, s, t):
                b = proj_block(s, t, 512)
                if cb < 2:
                    sa = next_stage()
                    op("act", lambda e, b=b, sa=sa: e.activation(out=SR[sa][0], in_=BK[b][:], func=AF.Identity),
                       reads=[("bk", b)] + OV, writes=[SR[sa][1]])
                    q_like_evac(b, t, cb * 4, True, qT, "qT", src=SR[sa][0], src_res=[SR[sa][1]])
                elif cb < 4:
                    sa = state_out(b, 512, 0, lambda t=t, cb=cb: kst_d[j][t * 128:(t + 1) * 128, (cb - 2) * 512:(cb - 1) * 512])
                    q_like_evac(b, t, (cb - 2) * 4, True, kT, "kT", src=SR[sa][0], src_res=[SR[sa][1]])
                elif cb < 6:
                    sa = state_out(b, 512, 0, lambda t=t, cb=cb: vst_d[j][t * 128:(t + 1) * 128, (cb - 4) * 512:(cb - 3) * 512])
                    op("dve", lambda e, sa=sa, t=t, cb=cb: e.tensor_copy(
                        out=Vp[:, t, (cb - 4) * 4:(cb - 3) * 4, 0:128], in_=SR[sa][0].rearrange("p (h d) -> p h d", d=128)),
                       reads=[SR[sa][1]], writes=[("Vp", t)])
                else:
                    op("act", lambda e, b=b, t=t, cb=cb: e.activation(out=gate[:, t, (cb - 6) * 512:(cb - 5) * 512], in_=BK[b][:], func=AF.Silu),
                       reads=[("bk", b)], writes=[("gate", t, cb - 6)])
                    op("dve", lambda e, t=t, cb=cb: e.tensor_tensor(
                        out=gate[:, t, (cb - 6) * 512:(cb - 5) * 512].rearrange("p (h d) -> p h d", d=128),
                        in0=gate[:, t, (cb - 6) * 512:(cb - 5) * 512].rearrange("p (h d) -> p h d", d=128),
                        in1=gs_bc[:, j, :].unsqueeze(1).broadcast_to([128, 4, 128]), op=ALU.mult),
                       reads=[("gate", t, cb - 6), ("gsj", j)], writes=[("gate", t, cb - 6)])

            s_first = w_take() if kind == 0 else None

            def first_block(t):
                if kind == 0:
                    do_block(0, s_first, t)

            def hT_transposes(t):
                pb = t % 2
                for g in range(2):
                    transposes_to(lambda ii, pb=pb, g=g: hb[:, pb, (g * 4 + ii) * 128:(g * 4 + ii + 1) * 128], 4, [("hb", pb)],
                                  lambda t=t, g=g: hT[:, g * 4:(g + 1) * 4, t * 128:(t + 1) * 128], [("hT", t)], copy_eng="act")

            for t in range(NT):
                pb = t % 2
                op("dve", lambda e, t=t: e.scalar_tensor_tensor(out=scrA[:], in0=xs[:, t, :], scalar=stat[:, 16 + t:17 + t],
                                                               in1=modv[:, D:2 * D], op0=ALU.mult, op1=ALU.mult),
                   reads=[("x", t), "rsx", "mBA"], writes=[("scrA", 0), ("scrA", 1)])
                op("dve", lambda e, pb=pb: e.tensor_tensor(out=hb[:, pb, :], in0=scrA[:], in1=modv[:, 0:D], op=ALU.add),
                   reads=[("scrA", 0), ("scrA", 1), "mBA"] + OV, writes=[("hb", pb)])
                if t >= 1:
                    hT_transposes(t - 1)
                    first_block(t - 1)
            hT_transposes(NT - 1)
            first_block(NT - 1)
            if kind == 0:
                w_done(s_first)

            if kind == 0:
                for cb in range(1, 8):
                    s = w_take()
                    for t in range(NT):
                        do_block(cb, s, t)
                    w_done(s)
                run_deferred(True)
            else:
                s0 = w_take()
                s1 = w_take()
                def mla_T(t):
                    pb = t % 2
                    transposes_to(lambda ii, pb=pb: hb[:, pb, ii * 128:(ii + 1) * 128], 3, [("hb", pb)],
                                  lambda t=t: qaT[:, 0:3, t * 128:(t + 1) * 128], [("qaT", t), "mBA"])
                    a = next_tb()
                    for ii in range(3):
                        op("pe", lambda e, a=a, ii=ii, pb=pb: e.transpose(out=TBk[a][:, ii, :], in_=hb[:, pb, 384 + ii * 128:384 + (ii + 1) * 128],
                                                                        identity=ident[:]),
                           reads=[("hb", pb), "ident"], writes=[("bk", a)])
                    op("dve", lambda e, a=a, t=t: e.tensor_copy(out=ckvT[:, 0:2, t * 128:(t + 1) * 128], in_=TBk[a][:, 0:2, :]),
                       reads=[("bk", a)], writes=[("ckvT", t)])
                    op("dve", lambda e, a=a, t=t: e.tensor_copy(out=kpeT[:, t * 128:(t + 1) * 128], in_=TBk[a][:, 2, :]),
                       reads=[("bk", a)], writes=[("kpeT", t)])

                for t in range(NT):
                    b0 = proj_block(s0, t, 384)
                    b1 = proj_block(s1, t, 320)
                    if t >= 1:
                        mla_T(t - 1)
                    sb0 = 56 if t % 2 == 0 else 72
                    op("act", lambda e, b0=b0: e.activation(out=q_r[:, 1, 0:384], in_=BK[b0][:, 0:384], func=AF.Square, accum_out=stat[:, sb0:sb0 + 1]),
                       reads=[("bk", b0)], writes=[("q_r", 1), ("ss_qa", t % 2)])
                    op("act", lambda e, b1=b1: e.activation(out=q_r[:, 1, 0:256], in_=BK[b1][:, 0:256], func=AF.Square, accum_out=stat[:, sb0 + 1:sb0 + 2]),
                       reads=[("bk", b1)], writes=[("q_r", 1), ("ss_kv", t % 2)])
                    op("pool", lambda e: e.tensor_scalar(out=stat[:, sb0 + 2:sb0 + 3], in0=stat[:, sb0:sb0 + 1], scalar1=1.0 / 384, scalar2=EPS, op0=ALU.mult, op1=ALU.add),
                       reads=[("ss_qa", t % 2)], writes=[("ms_qa", t % 2)])
                    op("pool", lambda e: e.tensor_scalar(out=stat[:, sb0 + 3:sb0 + 4], in0=stat[:, sb0 + 1:sb0 + 2], scalar1=1.0 / 256, scalar2=EPS, op0=ALU.mult, op1=ALU.add),
                       reads=[("ss_kv", t % 2)], writes=[("ms_kv", t % 2)])
                    op("pool", lambda e: e.tensor_tensor(out=stat[:, sb0 + 4:sb0 + 6], in0=stat[:, sb0 + 2:sb0 + 4], in1=cm05[:, 0:2], op=ALU.pow),
                       reads=[("ms_qa", t % 2), ("ms_kv", t % 2), "cm05"], writes=[("rs_qk", t % 2)])
                    pb = t % 2
                    op("dve", lambda e, b0=b0, pb=pb: e.scalar_tensor_tensor(out=hb[:, pb, 0:384], in0=BK[b0][:, 0:384], scalar=stat[:, sb0 + 4:sb0 + 5],
                                                                            in1=gqa_bc[:], op0=ALU.mult, op1=ALU.mult),
                       reads=[("bk", b0), ("rs_qk", t % 2), "gqa"] + OV, writes=[("hb", pb)])
                    sa = next_stage()
                    op("dve", lambda e, b1=b1, sa=sa: e.scalar_tensor_tensor(out=SR[sa][0][:, 0:256], in0=BK[b1][:, 0:256], scalar=stat[:, sb0 + 5:sb0 + 6],
                                                                            in1=gkva_bc[:], op0=ALU.mult, op1=ALU.mult),
                       reads=[("bk", b1), ("rs_qk", t % 2), "gkva"] + OV, writes=[SR[sa][1]])
                    op("sp", lambda e, sa=sa, t=t: e.dma_start(out=ckvst_d[j][t * 128:(t + 1) * 128, :], in_=SR[sa][0][:, 0:256]),
                       reads=[SR[sa][1]], writes=[dram_res()], dma=True)
                    op("dve", lambda e, sa=sa, pb=pb: e.tensor_copy(out=hb[:, pb, 384:640], in_=SR[sa][0][:, 0:256]),
                       reads=[SR[sa][1]], writes=[("hb", pb)])
                    kp = t % 2
                    op("act", lambda e, b1=b1, kp=kp: e.activation(out=kpe_st[:, kp, :], in_=BK[b1][:, 256:320], func=AF.Identity),
                       reads=[("bk", b1)], writes=[("kpe_st", kp)])
                    op("sp", lambda e, kp=kp, t=t: e.dma_start(out=kpest_d[j][t * 128:(t + 1) * 128, :], in_=kpe_st[:, kp, :]),
                       reads=[("kpe_st", kp)], writes=[dram_res()], dma=True)
                    rope_block(kpe_st[:, kp, :].unsqueeze(1), 1, t, hb[:, pb, 640:704].unsqueeze(1), [("kpe_st", kp)], [("hb", pb)], eng_mul="pool", eng_add="pool")
                    op("pool", lambda e, pb=pb: e.tensor_copy(out=hb[:, pb, 704:768], in_=hb[:, pb, 640:704]),
                       reads=[("hb", pb)], writes=[("hb", pb)])
                mla_T(NT - 1)
                w_done(s0)
                w_done(s1)
                for cb in range(2):
                    s = w_take()
                    for t in range(NT):
                        b = proj_block(s, t, 512)
                        op("act", lambda e, b=b, t=t, cb=cb: e.activation(out=gate[:, t, cb * 512:(cb + 1) * 512], in_=BK[b][:], func=AF.Silu),
                           reads=[("bk", b)], writes=[("gate", t, cb)])
                    w_done(s)
                for cb in range(3):
                    s = w_take()
                    for t in range(NT):
                        b = proj_block(s, t, 512, kc=3, lhs=qaT, lhs_res="qaT")
                        if cb == 2:
                            sa = next_stage()
                            op("act", lambda e, b=b, sa=sa: e.activation(out=SR[sa][0], in_=BK[b][:], func=AF.Identity),
                               reads=[("bk", b)] + OV, writes=[SR[sa][1]])
                            q_like_evac(b, t, cb * 4, True, qT, "qT", src=SR[sa][0], src_res=[SR[sa][1]])
                        else:
                            q_like_evac(b, t, cb * 4, False, qT, "qT")
                    w_done(s)
                run_deferred(True)
                kblocks = [(0, 512), (512, 512), (1024, 256)]
                for half in range(2):
                    s = w_take()
                    for hh in range(4):
                        h = half * 4 + hh
                        for (k0, kn) in kblocks:
                            b = next_bank6()
                            for c in range(2):
                                op("pe", lambda e, b=b, c=c, s=s, hh=hh, k0=k0, kn=kn: e.matmul(
                                    BK[b][:, 0:kn], lhsT=Wr[s][:, c, hh * 128:(hh + 1) * 128], rhs=ckvT[:, c, k0:k0 + kn],
                                    start=(c == 0), stop=(c == 1)),
                                   reads=[("w", s)] + [("ckvT", tt) for tt in range(k0 // 128, (k0 + kn) // 128)], writes=[("bk", b)])
                            op("dve", lambda e, b=b, h=h, k0=k0, kn=kn: e.tensor_copy(out=kT[:, h, k0:k0 + kn], in_=BK[b][:, 0:kn]),
                               reads=[("bk", b)], writes=[("kT", tt) for tt in range(k0 // 128, (k0 + kn) // 128)])
                    w_done(s)
                for vb in range(2):
                    s = w_take()
                    for kt in range(NKT):
                        b = next_bank6()
                        for c in range(2):
                            op("pe", lambda e, b=b, c=c, s=s, kt=kt: e.matmul(
                                BK[b][:], lhsT=ckvT[:, c, kt * 128:(kt + 1) * 128], rhs=Wr[s][:, c, :], start=(c == 0), stop=(c == 1)),
                               reads=[("w", s), ("ckvT", kt)], writes=[("bk", b)])
                        op("act", lambda e, b=b, kt=kt, vb=vb: e.activation(
                            out=Vp[:, kt, vb * 4:(vb + 1) * 4, 0:128], in_=BK[b][:].rearrange("p (h d) -> p h d", d=128), func=AF.Identity),
                           reads=[("bk", b)], writes=[("Vp", kt)])
                    w_done(s)

            chk(4)
            so = [w_take(), w_take()]
            tb_banks[:] = [6]
            barrier("B")

            scale = 0.125 if kind == 0 else (192.0 ** -0.5)
            if kind == 0:
                units = [(qb, (2 * p, 2 * p + 1), c) for qb in range(4) for p in range(4) for c in range(2)]
            else:
                units = [(qb, hp, 0) for qb in range(4) for hp in ((0, 2), (1, 3), (4, 6), (5, 7))]
            steps = [(u, kt) for u in range(len(units)) for kt in range(NKT)]
            LA = 2
            pending = {}
            SB = (0, 1, 7)

            def qz_of(h, c):
                r0 = (c * 64) if kind == 0 else ((h % 2) * 64)
                return ((qzA, 0) if r0 == 0 else (qzB, 1))

            def fill_qz(u):
                if u >= len(units):
                    return
                qb, heads, c = units[u]
                pp = u % 2
                q0 = qb * 256
                for sidx, h in enumerate(heads):
                    qz, zi = qz_of(h, c)
                    r0 = 0 if zi == 0 else 64
                    chunk = h if kind == 0 else 8 + h // 2
                    op("dve", lambda e, qz=qz, r0=r0, sidx=sidx, chunk=chunk: e.tensor_copy(
                        out=qz[r0:r0 + 64, pp, sidx, :], in_=qT[r0:r0 + 64, chunk, q0:q0 + 256]),
                       reads=[("qT", 2 * qb), ("qT", 2 * qb + 1)], writes=[("qz", zi, pp, sidx)])

            def S_step(si):
                u, kt = steps[si]
                qb, heads, c = units[u]
                sbk = SB[si % 3]
                sres = ("bk", sbk)
                q0 = qb * 256
                qres = [("qT", 2 * qb), ("qT", 2 * qb + 1)]
                pp = u % 2
                for sidx, h in enumerate(heads):
                    Sap = BK[sbk][:, sidx * 256:(sidx + 1) * 256]
                    qz, zi = qz_of(h, c)
                    zres = ("qz", zi, pp, sidx)
                    if kind == 0:
                        op("pe", lambda e, Sap=Sap, h=h, qz=qz, sidx=sidx: e.matmul(Sap, lhsT=kT[:, h, kt * 128:(kt + 1) * 128],
                                                                                   rhs=qz[:, pp, sidx, :], start=True, stop=True),
                           reads=[("kT", kt), zres], writes=[sres])
                    else:
                        op("pe", lambda e, Sap=Sap, h=h: e.matmul(Sap, lhsT=kT[:, h, kt * 128:(kt + 1) * 128], rhs=qT[:, h, q0:q0 + 256],
                                                                  start=True, stop=False),
                           reads=[("kT", kt)] + qres, writes=[sres])
                        op("pe", lambda e, Sap=Sap, qz=qz, sidx=sidx: e.matmul(Sap, lhsT=kpeT[:, kt * 128:(kt + 1) * 128],
                                                                              rhs=qz[:, pp, sidx, :], start=False, stop=True),
                           reads=[("kpeT", kt), zres], writes=[sres])
                pi = si % 3
                op("act", lambda e: e.activation(out=PT[:, pi, :], in_=BK[sbk][:], func=AF.Exp, bias=mb[:, qb * 10 + kt:qb * 10 + kt + 1], scale=scale),
                   reads=[sres, "mb"] + OV, writes=[("PT", pi)])

            def PV_step(si):
                u, kt = steps[si]
                qb, heads, c = units[u]
                pi = si % 3
                for sidx, h in enumerate(heads):
                    ob = 2 + 2 * (u % 2) + sidx
                    for half in range(2):
                        op("pe", lambda e, half=half, ob=ob, h=h, sidx=sidx: e.matmul(
                            BK[ob][:, half * 256:half * 256 + 129],
                            lhsT=PT[:, pi, sidx * 256 + half * 128:sidx * 256 + (half + 1) * 128],
                            rhs=Vp[:, kt, h, 0:129], start=(kt == 0 and half == 0),
                            stop=(kt == NKT - 1 and half == 1), skip_group_check=True),
                           reads=[("PT", pi), ("Vp", kt), "Vp_ones"], writes=[("bk", ob)])

            def unit_epilogue(u):
                qb, heads, c = units[u]
                for sidx, h in enumerate(heads):
                    ob = 2 + 2 * (u % 2) + sidx
                    rc0 = 32 + 2 * sidx
                    op("dve", lambda e, ob=ob, rc0=rc0: e.reciprocal(
                        out=stat[:, rc0:rc0 + 2], in_=BK[ob][:].rearrange("p (a c) -> p a c", a=2)[:, :, 128]),
                       reads=[("bk", ob)], writes=[("rc", sidx)])
                    if kind == 0 and c == 1:
                        op("dve", lambda e, rc0=rc0: e.tensor_scalar(out=stat[:, rc0:rc0 + 2], in0=stat[:, rc0:rc0 + 2],
                                                                    scalar1=neglam[:, j:j + 1], scalar2=None, op0=ALU.mult),
                           reads=[("rc", sidx), ("neglam", j)], writes=[("rc", sidx)])
                    for half in range(2):
                        oc = half * 256
                        if kind == 0 and c == 0:
                            op("dve", lambda e, ob=ob, rc0=rc0, sidx=sidx, half=half, oc=oc: e.tensor_scalar(
                                out=t0[:, sidx, half, :], in0=BK[ob][:, oc:oc + 128], scalar1=stat[:, rc0 + half:rc0 + half + 1],
                                scalar2=None, op0=ALU.mult),
                               reads=[("bk", ob), ("rc", sidx)] + OV, writes=[("t0", sidx, half)])
                        elif kind == 0:
                            op("dve", lambda e, ob=ob, rc0=rc0, sidx=sidx, half=half, oc=oc, h=h: e.scalar_tensor_tensor(
                                out=opre[:, half, h, :], in0=BK[ob][:, oc:oc + 128], scalar=stat[:, rc0 + half:rc0 + half + 1],
                                in1=t0[:, sidx, half, :], op0=ALU.mult, op1=ALU.add),
                               reads=[("bk", ob), ("rc", sidx), ("t0", sidx, half)] + OV, writes=[("opre", half, h)])
                            op("dve", lambda e, half=half, h=h: e.scalar_tensor_tensor(
                                out=q_r[:, 0, 0:128], in0=opre[:, half, h, :], scalar=1.0, in1=opre[:, half, h, :],
                                op0=ALU.mult, op1=ALU.mult, accum_out=stat[:, 40 + half * 8 + h:41 + half * 8 + h]),
                               reads=[("opre", half, h)], writes=[("q_r", 0), ("ssq", half, h)])
                        else:
                            op("dve", lambda e, ob=ob, rc0=rc0, half=half, oc=oc, h=h: e.tensor_scalar(
                                out=opre[:, half, h, :], in0=BK[ob][:, oc:oc + 128], scalar1=stat[:, rc0 + half:rc0 + half + 1],
                                scalar2=None, op0=ALU.mult),
                               reads=[("bk", ob), ("rc", sidx)] + OV, writes=[("opre", half, h)])

            def tail_stage1(qb):
                pcs = []
                for half in range(2):
                    t = 2 * qb + half
                    if kind == 0:
                        c0 = 24 + 4 * half
                        rcol = 64 + 8 * half
                        op("pool", lambda e, half=half, rcol=rcol: e.tensor_scalar(out=stat[:, rcol:rcol + 8],
                                                                                 in0=stat[:, 40 + half * 8:48 + half * 8],
                                                                                 scalar1=1.0 / 128, scalar2=EPS, op0=ALU.mult, op1=ALU.add),
                           reads=[("ssq", half, hh) for hh in range(8)], writes=[("s8", half)])
                        op("pool", lambda e, rcol=rcol: e.tensor_tensor(out=stat[:, rcol:rcol + 8], in0=stat[:, rcol:rcol + 8], in1=cm05[:, 0:8], op=ALU.pow),
                           reads=[("s8", half), "cm05"], writes=[("s8", half)])
                        o3 = opre[:, half, :, :]
                        pcs.append(lambda half=half, o3=o3, rcol=rcol: op(
                            "dve", lambda e: e.tensor_tensor(out=o3, in0=o3, in1=stat[:, rcol:rcol + 8].unsqueeze(2).broadcast_to([128, 8, 128]),
                                                             op=ALU.mult),
                            reads=[("s8", half)] + [("opre", half, hh) for hh in range(8)],
                            writes=[("opre", half, hh) for hh in range(8)]))
                        pcs.append(lambda half=half, t=t: op(
                            "dve", lambda e: e.tensor_tensor(out=hb[:, half, :], in0=opre[:, half, :, :].rearrange("p h d -> p (h d)"),
                                                             in1=gate[:, t, :], op=ALU.mult),
                            reads=[("opre", half, hh) for hh in range(8)] + [("gate", t, 0), ("gate", t, 1)] + OV,
                            writes=[("hb", half)]))
                    else:
                        pcs.append(lambda half=half, t=t: op(
                            "dve", lambda e: e.tensor_tensor(out=hb[:, half, :], in0=opre[:, half, :, :].rearrange("p h d -> p (h d)"),
                                                             in1=gate[:, t, :], op=ALU.mult),
                            reads=[("opre", half, hh) for hh in range(8)] + [("gate", t, 0), ("gate", t, 1)] + OV,
                            writes=[("hb", half)]))
                return pcs

            def tail_pieces(qb):
                pcs = []
                for half in range(2):
                    t = 2 * qb + half
                    for g in range(2):
                        pcs.append((lambda half=half, g=g, t=t: transposes_to(
                            lambda ii, half=half, g=g: hb[:, half, (g * 4 + ii) * 128:(g * 4 + ii + 1) * 128], 4, [("hb", half)],
                            lambda half=half, g=g: ogT[:, g * 4:(g + 1) * 4, half * 128:(half + 1) * 128], [("stage", g)]), 1))
                    for nb in range(2):
                        pcs.append((lambda half=half, nb=nb, t=t: wout_piece(t, nb, 0, half), 2))
                        pcs.append((lambda half=half, nb=nb, t=t: wout_piece(t, nb, 1, half), 3))
                return pcs

            def wout_piece(t, nb, part, half):
                for c in range(4 * part, 4 * part + 4):
                    op("pe", lambda e, nb=nb, c=c, half=half: e.matmul(BK[6][:], lhsT=ogT[:, c, half * 128:(half + 1) * 128], rhs=Wr[so[nb]][:, c, :],
                                                                     start=(c == 0), stop=(c == 7)),
                       reads=[("stage", 0), ("stage", 1), ("w", so[nb])], writes=[("bk", 6)])
                if part == 0:
                    return
                op("dve", lambda e, nb=nb: e.tensor_copy(out=scrA[:, nb * 512:(nb + 1) * 512], in_=BK[6][:]),
                   reads=[("bk", 6)], writes=[("scrA", nb)])
                op("dve", lambda e, nb=nb: e.scalar_tensor_tensor(
                    out=q_r[:, 0, :], in0=scrA[:, nb * 512:(nb + 1) * 512], scalar=1.0, in1=scrA[:, nb * 512:(nb + 1) * 512],
                    op0=ALU.mult, op1=ALU.mult, accum_out=stat[:, 36 + nb:37 + nb]),
                   reads=[("scrA", nb)], writes=[("q_r", 0), ("ssy", nb)])
                if nb == 0:
                    return
                op("pool", lambda e: e.tensor_tensor(out=stat[:, 38:39], in0=stat[:, 36:37], in1=stat[:, 37:38], op=ALU.add),
                   reads=[("ssy", 0), ("ssy", 1)], writes=["s38"])
                op("pool", lambda e: e.tensor_scalar(out=stat[:, 38:39], in0=stat[:, 38:39], scalar1=1.0 / D, scalar2=EPS, op0=ALU.mult, op1=ALU.add),
                   reads=["s38"], writes=["s38"])
                op("pool", lambda e: e.tensor_tensor(out=stat[:, 39:40], in0=stat[:, 38:39], in1=cm05[:, 0:1], op=ALU.pow),
                   reads=["s38", "cm05"], writes=["rsy"])
                op("dve", lambda e: e.scalar_tensor_tensor(out=scrA[:], in0=scrA[:], scalar=stat[:, 39:40], in1=modv[:, 2 * D:3 * D],
                                                           op0=ALU.mult, op1=ALU.mult),
                   reads=[("scrA", 0), ("scrA", 1), "rsy", "mG"], writes=[("scrA", 0), ("scrA", 1)])
                op("dve", lambda e, t=t: e.tensor_tensor(out=xs[:, t, :], in0=xs[:, t, :], in1=scrA[:], op=ALU.add),
                   reads=[("scrA", 0), ("scrA", 1), ("x", t)], writes=[("x", t)])
                if last:
                    op("sp", lambda e, t=t: e.dma_start(out=y_d[t * 128:(t + 1) * 128, :], in_=xs[:, t, :]),
                       reads=[("x", t)], writes=[dram_res()], dma=True)

            nsteps = len(steps)
            grp_open = [False]

            def run_pending(step):
                items = pending.pop(step, [])
                deferred = []
                for fn, mode in items:
                    if mode in (1, 2) and grp_open[0]:
                        deferred.append((fn, mode))
                        continue
                    fn()
                    if mode == 2:
                        grp_open[0] = True
                    elif mode == 3:
                        grp_open[0] = False
                if deferred:
                    pending[step + 1] = deferred + pending.get(step + 1, [])
            if i + 1 < NL:
                apcs = ada_pieces(i + 1, Gn, ["Gn"], land, ["land"], land, ["land"], 6, OV, split=True)
                sp_ = 16
                pending.setdefault(5, []).append(apcs[0])
                for b_ in range(6):
                    pending.setdefault(20 + 20 * b_, []).append(apcs[1 + 2 * b_])
                    pending.setdefault(21 + 20 * b_, []).append(apcs[2 + 2 * b_])
                pending.setdefault(135, []).append(apcs[13])
                pending.setdefault(150, []).append(apcs[14])
                pending.setdefault(150 + sp_ // 2, []).append((lambda li_=i: cache_loads(li_ + 1, landb, ["land"]), 0))
            for i in range(nsteps + LA):
                if DBG is not None and DBG >= 100 and i >= DBG - 100:
                    raise _Stop()
                jx = i - LA
                if i < nsteps:
                    if i == 0:
                        fill_qz(0)
                        fill_qz(1)
                    elif steps[i][1] == 0:
                        fill_qz(steps[i][0] + 1)
                    S_step(i)
                if jx >= 0:
                    PV_step(jx)
                    u, kt = steps[jx]
                    if kt == NKT - 1:
                        unit_epilogue(u)
                        qb = units[u][0]
                        if u + 1 == len(units) or units[u + 1][0] != qb:
                            p1 = tail_stage1(qb)
                            for k_, pc in enumerate(p1):
                                pending.setdefault(jx + 3 + 2 * k_, []).append((pc, 0))
                            for k_, pc in enumerate(tail_pieces(qb)):
                                pending.setdefault(jx + 24 + k_ + k_ // 2, []).append(pc)
                    run_pending(jx)
            while pending:
                run_pending(min(pending))
            w_done(so[0])
            w_done(so[1])

        try:
            for li in range(NL):
                layer(li)
        except _Stop:
            pass

        op("sp", None, reads=list(dram_list))
        P.emit()
    return nc


_NC_CACHE = {}


def _rope_tables():
    rows = T // 64
    row = np.repeat(np.arange(rows), 64).astype(np.float32)
    col = np.tile(np.arange(64), rows).astype(np.float32)
    inv = (np.float32(10000.0) ** (-np.arange(16, dtype=np.float32) / np.float32(16))).astype(np.float32)
    ang = np.concatenate([row[:, None] * inv, col[:, None] * inv], axis=-1).astype(np.float32)
    return np.cos(ang).astype(np.float32), np.sin(ang).astype(np.float32)


def kernel(x_prompt, x_sample, cache_diff_k, cache_diff_v, cache_mla_ckv, cache_mla_kpe,
           c, c_ctx, w_ada, b_ada, g_pre, g_post, w_out,
           da_w_in, da_lam_q1, da_lam_k1, da_lam_q2, da_lam_k2, da_g_sub,
           mla_w_in, mla_g_qa, mla_w_qb, mla_g_kva, mla_w_kvb, _NL=4):
    f = lambda a: np.ascontiguousarray(np.asarray(a, dtype=np.float32))
    x_prompt, x_sample = f(x_prompt), f(x_sample)
    if _NL not in _NC_CACHE:
        _NC_CACHE[_NL] = build_nc(_NL)
    nc = _NC_CACHE[_NL]
    cosv, sinv = _rope_tables()
    ones = np.ones_like(cosv)
    zeros = np.zeros_like(sinv)
    pq = np.concatenate([np.arange(h * 192, h * 192 + 128) for h in range(8)] +
                        [np.arange(h * 192 + 128, (h + 1) * 192) for h in range(8)])
    pkv = np.concatenate([np.arange(h * 256, h * 256 + 128) for h in range(8)] +
                         [np.arange(h * 256 + 128, (h + 1) * 256) for h in range(8)])
    shared = {
        "w_ada": f(w_ada), "b_ada": f(b_ada), "g_pre": f(g_pre), "g_post": f(g_post), "w_out": f(w_out),
        "da_w_in": f(da_w_in),
        "lamv": f(np.stack([np.asarray(da_lam_q1), np.asarray(da_lam_k1), np.asarray(da_lam_q2), np.asarray(da_lam_k2)], axis=1)),
        "da_g_sub": f(da_g_sub), "mla_w_in": f(mla_w_in), "mla_g_qa": f(mla_g_qa),
        "mla_w_qb": f(np.asarray(mla_w_qb)[:, :, pq]), "mla_g_kva": f(mla_g_kva),
        "mla_w_kvb": f(np.asarray(mla_w_kvb)[:, :, pkv]),
    }
    mb_lat = np.zeros((40,), np.float32)
    mb_ctx = np.full((4, 10), NEG, np.float32)
    for qb in range(4):
        mb_ctx[qb, 2 * qb:2 * qb + 2] = 0.0
    mb_ctx = mb_ctx.reshape(40)
    cdk = f(cache_diff_k).reshape(4, 2, 256, 1024)
    cdv = f(cache_diff_v).reshape(4, 2, 256, 1024)
    cckv = f(cache_mla_ckv)
    ckpe = f(cache_mla_kpe)
    c = f(c)
    c_ctx = f(c_ctx)
    in_maps = []
    for core in range(8):
        m = dict(shared)
        if core < 4:
            b = core
            m.update(x=x_sample[b], cond=c[b], cosT=cosv, sinT=sinv, mb=mb_lat,
                     cdk=cdk[b], cdv=cdv[b], cckv=cckv[b], ckpe=ckpe[b])
        else:
            g = core - 4
            m.update(x=np.ascontiguousarray(x_prompt[4 * g:4 * g + 4].reshape(T, D)), cond=c_ctx, cosT=ones, sinT=zeros, mb=mb_ctx,
                     cdk=np.zeros((2, 256, 1024), np.float32), cdv=np.zeros((2, 256, 1024), np.float32),
                     cckv=np.zeros((2, 256, 256), np.float32), ckpe=np.zeros((2, 256, 64), np.float32))
        in_maps.append(m)
    res = run_bass_kernel_spmd(nc, in_maps, core_ids=list(range(8)))
    r = res.results
    y_sample = np.stack([r[b]["y"] for b in range(4)], axis=0).astype(np.float32)
    y_prompt = np.concatenate([r[4 + g]["y"].reshape(4, 256, D) for g in range(4)], axis=0).astype(np.float32)

    def gather(name, last):
        outs = []
        for g in range(4):
            a = r[4 + g][name].reshape(2, 4, 256, last)
            outs.append(np.transpose(a, (1, 0, 2, 3)))
        return np.concatenate(outs, axis=0).astype(np.float32)

    sdk = gather("kst", 1024).reshape(16, 2, 256, 8, 2, 64)
    sdv = gather("vst", 1024).reshape(16, 2, 256, 8, 128)
    sckv = gather("ckvst", 256)
    skpe = gather("kpest", 64)
    return (y_prompt, y_sample, sdk, sdv, sckv, skpe)
```
